# Optimizing a Trainium2 kernel written in Bass

```python
import jax
import jax.numpy as jnp
from jax import lax
import numpy as np


D_MODEL = 4096
BATCH = 2
SEQ = 8192
DEPTH = 1

N_META = 16
HEAD_SIZE = 64
D_RWKV = D_MODEL
N_RWKV_HEADS = D_RWKV // HEAD_SIZE
DECAY_LORA = max(32, int(round(1.8 * D_RWKV ** 0.5 / 32)) * 32)
ICLR_LORA = max(32, int(round(1.8 * D_RWKV ** 0.5 / 32)) * 32)
D_CONV = D_MODEL
CONV_WIDTH = 3
D_FF = 4 * D_MODEL
NORM_EPS = 1e-6
GN_EPS = 64e-5
N_RWKV_COLS = 3 * D_RWKV + DECAY_LORA + ICLR_LORA
N_CONV_COLS = 3 * D_CONV
N_IN_COLS = N_RWKV_COLS + N_CONV_COLS + D_RWKV + D_CONV

kernel_name = 'hybrid_rwkv7_shortconv_sqrelu_block'


def rms_norm(x, g):
    xf = x.astype(jnp.float32)
    y = xf * lax.rsqrt(jnp.mean(xf * xf, axis=-1, keepdims=True) + NORM_EPS)
    return (y * g.astype(jnp.float32)).astype(x.dtype)


def token_shift(z):
    return jnp.pad(z, ((0, 0), (1, 0), (0, 0)))[:, :-1]


def rwkv7_scan(r, decay, k, v, kk, a):
    bsz, _, n_heads, n = r.shape

    def step(S, inp):
        r_t, w_t, k_t, v_t, kk_t, a_t = inp
        sa = jnp.einsum('bhij,bhj->bhi', S, -kk_t)
        S = (S * w_t[:, :, None, :]
             + sa[..., None] * (kk_t * a_t)[:, :, None, :]
             + v_t[..., None] * k_t[:, :, None, :])
        y_t = jnp.einsum('bhij,bhj->bhi', S, r_t)
        return S, y_t

    S0 = jnp.zeros((bsz, n_heads, n, n), jnp.float32)
    seq = tuple(jnp.moveaxis(t, 1, 0) for t in (r, decay, k, v, kk, a))
    _, y = lax.scan(step, S0, seq)
    return jnp.moveaxis(y, 0, 1)


def rwkv7_mixer(p, shift_mu, w0, w2, a0, a2, k_k, k_a, r_k, ln_w, ln_b):
    bsz, n_pos, _ = p.shape
    f = lambda t: t.astype(jnp.float32)
    z = f(p + (token_shift(p) - p) * shift_mu)
    r, k, v, zw, za = jnp.split(z, [D_RWKV, 2 * D_RWKV, 3 * D_RWKV, 3 * D_RWKV + DECAY_LORA], axis=-1)
    w_log = -jax.nn.softplus(-(f(w0) + jnp.tanh(zw) @ f(w2))) - 0.5
    decay = jnp.exp(-jnp.exp(w_log))
    a = jax.nn.sigmoid(f(a0) + za @ f(a2))
    heads = lambda t: t.reshape(bsz, n_pos, N_RWKV_HEADS, HEAD_SIZE)
    kk = heads(k * f(k_k))
    kk = kk / jnp.maximum(jnp.sqrt(jnp.sum(kk * kk, axis=-1, keepdims=True)), 1e-12)
    k = k * (1.0 + (a - 1.0) * f(k_a))
    r, k, v, decay, a = map(heads, (r, k, v, decay, a))
    y = rwkv7_scan(r, decay, k, v, kk, a)
    mu = jnp.mean(y, axis=-1, keepdims=True)
    var = jnp.mean(jnp.square(y - mu), axis=-1, keepdims=True)
    y = ((y - mu) * lax.rsqrt(var + GN_EPS)).reshape(bsz, n_pos, D_RWKV) * f(ln_w) + f(ln_b)
    bonus = jnp.sum(r * k * f(r_k), axis=-1, keepdims=True) * v
    return (y + bonus.reshape(bsz, n_pos, D_RWKV)).astype(p.dtype)


def short_conv_mixer(p, conv_w):
    b_gate, c_gate, h = jnp.split(p, [D_CONV, 2 * D_CONV], axis=-1)
    u = c_gate * h
    conv = lax.conv_general_dilated(
        u, conv_w[:, None, :].astype(u.dtype), window_strides=(1,),
        padding=[(CONV_WIDTH - 1, 0)], dimension_numbers=('NWC', 'WIO', 'NWC'),
        feature_group_count=D_CONV)
    return b_gate * conv


def setup_inputs(seed: int = 0) -> dict:
    key = jax.random.key(seed)
    ks = jax.random.split(key, 20)
    f32 = jnp.float32

    def nrm(k, shape, scale):
        return jax.random.normal(k, shape, f32) * scale

    L = DEPTH
    return {
        'x': nrm(ks[0], (BATCH, SEQ, D_MODEL), 1.0),
        'meta_tokens': nrm(ks[1], (N_META, D_MODEL), 1.0),
        'norm_mix_g': 1.0 + nrm(ks[2], (L, D_MODEL), 0.02),
        'w_in': nrm(ks[3], (L, D_MODEL, N_IN_COLS), D_MODEL ** -0.5),
        'rwkv_shift_mu': jax.random.uniform(ks[4], (L, N_RWKV_COLS), f32),
        'rwkv_w0': jax.random.uniform(ks[5], (L, D_RWKV), f32, -6.0, -1.0),
        'rwkv_w2': nrm(ks[6], (L, DECAY_LORA, D_RWKV), 0.5 * DECAY_LORA ** -0.5),
        'rwkv_a0': nrm(ks[7], (L, D_RWKV), 0.1),
        'rwkv_a2': nrm(ks[8], (L, ICLR_LORA, D_RWKV), 0.5 * ICLR_LORA ** -0.5),
        'rwkv_k_k': 0.85 + nrm(ks[9], (L, D_RWKV), 0.02),
        'rwkv_k_a': 1.0 + nrm(ks[10], (L, D_RWKV), 0.02),
        'rwkv_r_k': nrm(ks[11], (L, N_RWKV_HEADS, HEAD_SIZE), 0.1),
        'rwkv_ln_w': 1.0 + nrm(ks[12], (L, D_RWKV), 0.02),
        'rwkv_ln_b': nrm(ks[13], (L, D_RWKV), 0.02),
        'conv_w': nrm(ks[14], (L, CONV_WIDTH, D_CONV), CONV_WIDTH ** -0.5),
        'w_out': nrm(ks[15], (L, D_RWKV + D_CONV, D_MODEL), (D_RWKV + D_CONV) ** -0.5),
        'norm_mlp_g': 1.0 + nrm(ks[16], (L, D_MODEL), 0.02),
        'w_up': nrm(ks[17], (L, D_MODEL, D_FF), D_MODEL ** -0.5),
        'w_down': nrm(ks[18], (L, D_FF, D_MODEL), D_FF ** -0.5),
        'norm_final_g': 1.0 + nrm(ks[19], (D_MODEL,), 0.02),
    }


def reference(x, meta_tokens, norm_mix_g, w_in, rwkv_shift_mu, rwkv_w0, rwkv_w2, rwkv_a0,
              rwkv_a2, rwkv_k_k, rwkv_k_a, rwkv_r_k, rwkv_ln_w, rwkv_ln_b, conv_w, w_out,
              norm_mlp_g, w_up, w_down, norm_final_g):
    bsz = x.shape[0]
    meta = jnp.broadcast_to(meta_tokens.astype(x.dtype)[None], (bsz, N_META, D_MODEL))
    h = jnp.concatenate([meta, x], axis=1)
    split_pts = [N_RWKV_COLS, N_RWKV_COLS + N_CONV_COLS, N_RWKV_COLS + N_CONV_COLS + D_RWKV]
    for layer in range(DEPTH):
        u = rms_norm(h, norm_mix_g[layer])
        p = u @ w_in[layer]
        p_a, p_b, g_a, g_b = jnp.split(p, split_pts, axis=-1)
        y_a = rwkv7_mixer(p_a, rwkv_shift_mu[layer], rwkv_w0[layer], rwkv_w2[layer],
                          rwkv_a0[layer], rwkv_a2[layer], rwkv_k_k[layer], rwkv_k_a[layer],
                          rwkv_r_k[layer], rwkv_ln_w[layer], rwkv_ln_b[layer])
        y_b = short_conv_mixer(p_b, conv_w[layer])
        merged = jnp.concatenate([jax.nn.sigmoid(g_a) * y_a, jax.nn.sigmoid(g_b) * y_b], axis=-1)
        h = h + merged @ w_out[layer]
        u = rms_norm(h, norm_mlp_g[layer])
        h = h + jnp.square(jax.nn.relu(u @ w_up[layer])) @ w_down[layer]
    out = rms_norm(h, norm_final_g)
    return out[:, N_META:]
```

```python
import numpy as np
import ml_dtypes
import concourse.bass as bass
import concourse.mybir as mybir
from concourse.bass_utils import run_bass_kernel_spmd

F32 = mybir.dt.float32
BF16 = mybir.dt.bfloat16
AF = mybir.ActivationFunctionType
ALU = mybir.AluOpType
NPAR = 13
CDEC = 0.6065306597126334


class Cfg:
    def __init__(s, D, SEQ, PK):
        s.D = D; s.SEQ = SEQ; s.KC = D // 128; s.NH = D // 64; s.NP = s.NH // 2; s.HPC = s.NP // 4
        s.DFF = 4 * D; s.FR = s.DFF // 8; s.FRC = s.FR // 128
        s.QT = SEQ // 4; s.NT2 = s.QT // 512; s.NT1 = SEQ // 512
        s.NCB = 2 + 8 * s.HPC; s.MC = 2 * s.HPC * 128
        s.KO = 2 * s.KC; s.KOl = s.KO // 4; s.HPR = 2; s.CH = 524288; s.PK = PK; s.RPP = PK // s.KOl; s.NKQ = s.KO // PK
        s.PKu = min(PK, s.KC); s.NKH = s.KC // s.PKu
        s.SPW = min(512, s.FR); s.NSP = s.FR // s.SPW
        s.NBO = D // 512
        s.NR = 3 * D + 256
        s.NPV = 2 + NPAR * s.HPC + 2 * s.KC
        s.WS = max(PK * 512, s.PKu * s.SPW, s.FRC * 512, s.KC * 128)


FULL = Cfg(4096, 8192, 16)


class Prog:
    ENGS = ["pe", "act", "dve", "pool", "sp"]

    def __init__(self):
        self.streams = {e: [] for e in self.ENGS}
        self.cnt = {}
        self.lastw = {}
        self.readers = {}
        self.waited = {e: {} for e in self.ENGS}

    @staticmethod
    def key(k):
        if isinstance(k, str):
            return k
        if hasattr(k, "tensor"):
            return k.tensor.name
        return k.name

    def emit(self, eng, fn, w=(), r=(), sig=None, inc=1):
        sig = sig or ("e_" + eng)
        deps = {}

        def add(d):
            if d is not None:
                s, v = d
                if s.startswith("d_") or s.startswith("cc"):
                    v = self.cnt[s]
                deps[s] = max(deps.get(s, 0), v)

        rk = [self.key(k) for k in r]
        wk = [self.key(k) for k in w]
        for k in rk:
            add(self.lastw.get(k))
        for k in wk:
            add(self.lastw.get(k))
            for s, v in self.readers.get(k, {}).items():
                add((s, v))
        waits = []
        for s, v in deps.items():
            if s == "e_pe" and eng == "pe":
                continue
            if self.waited[eng].get(s, 0) < v:
                waits.append((s, v))
                self.waited[eng][s] = v
        self.cnt[sig] = self.cnt.get(sig, 0) + inc
        val = self.cnt[sig]
        self.streams[eng].append((waits, fn, sig, inc))
        for k in rk:
            d = self.readers.setdefault(k, {})
            d[sig] = max(d.get(sig, 0), val)
        for k in wk:
            self.lastw[k] = (sig, val)
            self.readers[k] = {}
        return val

    def wait_all(self, eng, sig):
        v = self.cnt.get(sig, 0)
        if self.waited[eng].get(sig, 0) < v:
            self.waited[eng][sig] = v
            self.streams[eng].append(([(sig, v)], None, None, 0))

    def finish_phase(self):
        waits = [(s, v) for s, v in self.cnt.items() if self.waited["sp"].get(s, 0) < v and s != "e_sp"]
        for s, v in waits:
            self.waited["sp"][s] = v
        self.streams["sp"].append((waits, None, None, 0))

    def replay(self, nc, sems):
        streams = self.streams
        self.streams = {e: [] for e in self.ENGS}

        def run(e, items):
            for waits, fn, sig, inc in items:
                for s, v in waits:
                    e.wait_ge(sems[s], v)
                if fn is not None:
                    ins = fn(e)
                    ins.then_inc(sems[sig], inc)

        with nc.Block() as block:
            @block.tensor
            def _(e):
                run(e, streams["pe"])

            @block.scalar
            def _(e):
                run(e, streams["act"])

            @block.vector
            def _(e):
                run(e, streams["dve"])

            @block.gpsimd
            def _(e):
                run(e, streams["pool"])

            @block.sync
            def _(e):
                run(e, streams["sp"])


SEM_NAMES = ["e_pe", "e_act", "e_dve", "e_pool", "e_sp", "d_ld", "d_st", "d_w0", "d_w1", "d_w2", "d_w3", "d_x", "d_cast",
             "cc", "d_out", "d_u"]


def build(cfg, stop_after=9, SKIP_SCAN=False):
    c = cfg
    D, KC, HPC, QT, SEQ = c.D, c.KC, c.HPC, c.QT, c.SEQ
    nc = bass.Bass("TRN2", target_bir_lowering=False)
    P = Prog()

    def din(name, shape, dt=F32):
        return nc.dram_tensor(name, list(shape), dt, kind="ExternalInput").ap()

    x_d = din("x", [QT, D]); xb_d = din("xb", [SEQ, D]); meta_d = din("meta", [16, D])
    win_d = din("win", [c.NCB * 128, KC * 128])
    wo_d = din("wo", [c.NBO * 128, c.KOl * 512])
    wu_d = din("wu", [c.HPR * c.NSP * c.NKH * 128, c.PKu * c.SPW])
    wd_d = din("wd", [c.HPR * c.NBO * 128, c.FRC * 512])
    pv_d = din("pv", [128, c.NPV]); w2_d = din("w2", [128, HPC * 128]); a2_d = din("a2", [128, HPC * 128])
    gf_d = din("gf", [1, D]); cst_d = din("cst", [128, 1024])
    out_d = nc.dram_tensor("out", [QT, D], F32, kind="ExternalOutput").ap()

    def dint(name, shape, dt=BF16):
        return nc.dram_tensor(name, list(shape), dt)

    winb = dint("winb", [c.NCB * 128, KC * 128])
    u_loc = dint("u_loc", [(SEQ // 512) * 128, KC * 512])
    u_meta = dint("u_meta", [128, KC * 128])
    wo_loc = dint("wo_loc", [c.NBO * 128, c.KOl * 512]); wo_all = dint("wo_all", [4 * c.NBO * 128, c.KOl * 512])
    wu_loc = dint("wu_loc", [c.HPR * c.NSP * c.NKH * 128, c.PKu * c.SPW])
    wu_all = dint("wu_all", [8 * c.NSP * c.NKH * 128, c.PKu * c.SPW])
    wd_loc = dint("wd_loc", [c.HPR * c.NBO * 128, c.FRC * 512]); wd_all = dint("wd_all", [8 * c.NBO * 128, c.FRC * 512])
    m_loc = dint("m_loc", [4 * c.NT2 * 128, 2 * HPC * 512]); m_all = dint("m_all", [16 * c.NT2 * 128, 2 * HPC * 512])

    from contextlib import ExitStack
    es = ExitStack()
    sems = {n: es.enter_context(nc.semaphore(n)) for n in SEM_NAMES}

    def sb(stack, name, shape, dt=F32):
        return stack.enter_context(nc.sbuf_tensor("s_" + name, list(shape), dt))

    def ps(stack, name, shape, dt=F32):
        return stack.enter_context(nc.psum_tensor(name, list(shape), dt))

    def mm(out, lhsT, rhs, w, r, start=True, stop=True):
        P.emit("pe", lambda e: e.matmul(out, lhsT=lhsT, rhs=rhs, start=start, stop=stop), w=w, r=r)

    def trp(out, in_, ident, w, r):
        P.emit("pe", lambda e: e.transpose(out, in_, ident), w=w, r=r)

    def act(out, in_, func, w, r, bias=None, scale=None, accum=None):
        kw = {}
        if bias is not None: kw["bias"] = bias
        if scale is not None: kw["scale"] = scale
        if accum is not None: kw["accum_out"] = accum
        P.emit("act", lambda e: e.activation(out, in_, func, **kw), w=w, r=r)

    def tt(eng, out, in0, in1, op, w, r):
        P.emit(eng, lambda e: e.tensor_tensor(out, in0, in1, op), w=w, r=r)

    def ts(eng, out, in0, s1, op0, w, r, s2=None, op1=None):
        if op1 is None:
            P.emit(eng, lambda e: e.tensor_scalar(out, in0, s1, None, op0), w=w, r=r)
        else:
            P.emit(eng, lambda e: e.tensor_scalar(out, in0, s1, s2, op0, op1), w=w, r=r)

    def stt(out, in0, scalar, in1, op0, op1, w, r):
        P.emit("dve", lambda e: e.scalar_tensor_tensor(out, in0, scalar, in1, op0, op1), w=w, r=r)

    def cp(eng, out, in_, w, r):
        if eng == "act":
            P.emit("act", lambda e: e.copy(out, in_), w=w, r=r)
        else:
            P.emit(eng, lambda e: e.tensor_copy(out, in_), w=w, r=r)

    def dma(out, in_, w, r, sig, eng="sp"):
        P.emit(eng, lambda e: e.dma_start(out=out, in_=in_), w=w, r=r, sig=sig, inc=16)

    def allgather(in_t, out_t, groups):
        P.emit("pool", lambda e: e.collective_compute("AllGather", ALU.bypass, replica_groups=groups,
                                                      ins=[in_t.ap().opt()], outs=[out_t.ap().opt()]),
               w=[out_t.name], r=[in_t.name], sig="cc", inc=1)

    G4 = [[0, 1, 2, 3], [4, 5, 6, 7]]

    def rcof(R, C):
        return min(R, max(1, c.CH // C))

    def gather_chunked(loc_t, all_t, R, C):
        rc = rcof(R, C)
        for k in range(R // rc):
            P.emit("pool", lambda e, k=k: e.collective_compute(
                "AllGather", ALU.bypass, replica_groups=G4,
                ins=[loc_t[k * rc:(k + 1) * rc, :].opt()], outs=[all_t[k * 4 * rc:(k + 1) * 4 * rc, :].opt()]),
                w=[all_t.name], r=[loc_t.name], sig="cc", inc=1)

    def gsegs(r, i0, n, R, C):
        rc = rcof(R, C)
        out = []
        i = i0
        while i < i0 + n:
            k, o = divmod(i, rc)
            nr = min(rc - o, i0 + n - i)
            out.append((i - i0, k * 4 * rc + r * rc + o, nr))
            i += nr
        return out

    def gload(dst_tile, col0, ncols, all_t, r, i0, R, C, sg):
        for po, src, nr in gsegs(r, i0, 128, R, C):
            dma(dst_tile[po:po + nr, col0:col0 + ncols], all_t[src:src + nr, :], [dst_tile], [all_t.name], sg)

    G8 = [list(range(8))]

    pv = sb(es, "pv", [128, c.NPV]); om = sb(es, "om", [128, c.NPV])
    cst = sb(es, "cst", [128, 1024])
    cstb = sb(es, "cstb", [128, 1024], BF16)
    eps1 = sb(es, "eps1", [128, 1]); eps2 = sb(es, "eps2", [128, 1])
    ident = cst[:, 0:128]
    blk1 = cst[:, 128:256]
    blka = cst[:, 256:384]
    idblk = cst[:, 384:448]
    identb = cstb[:, 0:128]
    blk1b = cstb[:, 128:256]
    m_sl = cstb[:, 512:640]
    m_su = cstb[:, 640:768]
    m_ui = cstb[:, 768:896]
    m_suui = cstb[:, 640:896]

    psb = [ps(es, f"psb{i}", [128, 512]) for i in range(7)]
    pstr = ps(es, "pstr", [128, 1024], BF16)
    PJ = psb[0:4]; SA = psb[4]; SB_ = psb[5]; SC = psb[6]

    dma(pv[:], pv_d, [pv], [], "d_ld")
    dma(cst[:], cst_d, [cst], [], "d_ld")
    cp("dve", cstb[:], cst[:], [cstb], [cst])
    ts("dve", om[:], pv[:], -1.0, ALU.mult, [om], [pv], s2=1.0, op1=ALU.add)
    P.emit("dve", lambda e: e.memset(eps1[:], 1e-6), w=[eps1])
    P.emit("dve", lambda e: e.memset(eps2[:], 64e-5), w=[eps2])

    GMIX = 2 + NPAR * HPC
    GMLP = GMIX + KC

    def rms_to_uT(xs, gcol0, uT, col0, junk, ss, rt, rstd, diag, uTk=None, junkk=None):
        uTk = uTk or uT
        junkk = junkk or junk
        act(junk, xs, AF.Square, [junkk, ss], [xs], accum=ss)
        act(rt, ss, AF.Sqrt, [rt], [ss, eps1], bias=eps1[:, 0:1], scale=1.0 / D)
        P.emit("dve", lambda e: e.reciprocal(rstd, rt), w=[rstd], r=[rt])
        ts("dve", diag, ident, rstd[:, 0:1], ALU.mult, [diag], [cst, rstd])
        for kc in range(KC):
            bank = PJ[kc % 4]
            reg = bank[:, 0:128]
            key = bank
            mm(reg, xs[:, kc * 128:(kc + 1) * 128], diag, [key], [xs, diag])
            act(uT[:, kc, col0:col0 + 128], reg, AF.Copy, [uTk], [key, pv],
                scale=pv[:, gcol0 + kc:gcol0 + kc + 1])
        return rstd

    s1 = ExitStack()
    NT = 512
    uT = sb(s1, "uT", [128, KC, NT], BF16)
    xs = sb(s1, "xs", [128, D])
    junk = sb(s1, "junk", [128, D], BF16)
    ss = sb(s1, "ss", [128, 1]); rt = sb(s1, "rt", [128, 1]); rstd = sb(s1, "rstd", [128, 1])
    diag = sb(s1, "diag", [128, 128])

    ncast = [0]

    def cast2d(dst_t, src_ap, rows, cols):
        for r0 in range(0, rows, 128):
            for c0 in range(0, cols, 2048):
                cw = min(2048, cols - c0)
                dma(dst_t[r0:r0 + 128, c0:c0 + cw], src_ap[r0:r0 + 128, c0:c0 + cw], [dst_t.name], [], "d_cast", eng="pool")
                ncast[0] += 1
                if ncast[0] % 16 == 0:
                    P.wait_all("pool", "d_cast")

    cast2d(winb, win_d, c.NCB * 128, KC * 128)
    cast2d(wo_loc, wo_d, c.NBO * 128, c.KOl * 512)
    cast2d(wu_loc, wu_d, c.HPR * c.NSP * c.NKH * 128, c.PKu * c.SPW)
    cast2d(wd_loc, wd_d, c.HPR * c.NBO * 128, c.FRC * 512)

    for t4 in range(SEQ // 512):
        for s in range(4):
            r0 = t4 * 512 + s * 128
            dma(xs[:], xb_d[r0:r0 + 128, :], [xs], [], "d_x")
            rms_to_uT(xs[:], GMIX, uT, s * 128, junk[:], ss[:], rt[:], rstd[:], diag[:])
        dma(u_loc[t4 * 128:(t4 + 1) * 128, :], uT[:].rearrange("p k t -> p (k t)"), ["u_loc"], [uT], "d_u")
    P.emit("pool", lambda e: e.memset(xs[:], 0.0), w=[xs])
    dma(xs[112:128, :], meta_d, [xs], [], "d_x")
    rms_to_uT(xs[:], GMIX, uT, 0, junk[:], ss[:], rt[:], rstd[:], diag[:])
    dma(u_meta.ap().rearrange("p (k t) -> p k t", t=128), uT[:, :, 0:128], ["u_meta"], [uT], "d_u")
    P.finish_phase()
    P.replay(nc, sems)
    s1.close()
    if stop_after == 0:
        es.close()
        return nc

    s1 = ExitStack()
    uT = sb(s1, "uT1", [128, KC, NT], BF16)
    wsl = [sb(s1, f"wsl{i}", [128, KC * 128], BF16) for i in range(4)]
    w2b = sb(s1, "w2b", [128, HPC * 128], BF16); a2b = sb(s1, "a2b", [128, HPC * 128], BF16)
    dma(w2b[:], w2_d, [w2b], [], "d_cast", eng="pool")
    dma(a2b[:], a2_d, [a2b], [], "d_cast", eng="pool")

    def f32t(name, n=NT + 2):
        return sb(s1, name, [128, n])

    def b16t(name, n=NT):
        return sb(s1, name, [128, n], BF16)

    pb = f32t("pb"); tmp = f32t("tmp"); rz = f32t("rz"); kz = f32t("kz"); vz = f32t("vz")
    zwm = f32t("zwm"); sgA = f32t("sgA"); Sc = f32t("Sc"); Sx = f32t("Sx"); aa = f32t("aa")
    kkr = f32t("kkr"); kk = f32t("kk"); k2 = f32t("k2"); bn = f32t("bn"); t2 = f32t("t2")
    eP = f32t("eP"); ePx = f32t("ePx"); eN = f32t("eN"); yy = f32t("yy"); yc = f32t("yc"); t3 = f32t("t3")
    ub = f32t("ub"); hb = f32t("hb"); acc = f32t("acc")
    eEnd = sb(s1, "eEnd", [128, 4])
    ones = f32t("ones")
    tw = b16t("tw"); zab = b16t("zab"); sqb = b16t("sqb"); vb = b16t("vb")
    KR = sb(s1, "KR", [128, 4, 2, 128], BF16)
    kt = b16t("kt"); bnt = b16t("bnt"); kh = b16t("kh"); bh = b16t("bh")
    mA = b16t("mA"); mB = b16t("mB")
    N_ = [sb(s1, f"Nn{i}", [128, 2, 128], BF16) for i in range(2)]
    Q_ = [sb(s1, f"Qq{i}", [128, 2, 2, 128], BF16) for i in range(2)]
    ArT = sb(s1, "ArT", [128, 2, 128], BF16)
    BB = sb(s1, "BB", [128, 2, 2, 128], BF16)
    TM = sb(s1, "TM", [128, 4, 128], BF16)
    XK = sb(s1, "XK", [128, 2, 128], BF16)
    WZ = sb(s1, "WZ", [128, 2, 128], BF16)
    GT = sb(s1, "GT", [128, 128], BF16)
    Hs = sb(s1, "Hs", [128, 128])
    t4 = f32t("t4")
    KRm = sb(s1, "KRm", [128, 4, 2, 128], BF16)
    bntm = sb(s1, "bntm", [128, 2, NT], BF16); ktm = sb(s1, "ktm", [128, 2, NT], BF16)
    RpT = sb(s1, "RpT", [128, 128], BF16)
    M32 = sb(s1, "M32", [128, HPC, 128]); Mb = sb(s1, "Mb", [128, HPC, 128], BF16)
    car = sb(s1, "car", [128, 2 + 4 * HPC, 2])

    P.emit("pool", lambda e: e.memset(M32[:], 0.0), w=[M32])
    P.emit("pool", lambda e: e.memset(Mb[:], 0.0), w=[Mb])
    P.emit("pool", lambda e: e.memset(car[:], 0.0), w=[car])
    P.emit("pool", lambda e: e.memset(ones[:], 1.0), w=[ones])

    wslot_i = [0]

    def load_w_cb(cb):
        i = wslot_i[0] % 4
        wslot_i[0] += 1
        dma(wsl[i][:, 0:KC * 128], winb[cb * 128:(cb + 1) * 128, :], [wsl[i]], ["winb"], f"d_w{i}")
        return wsl[i]

    def proj(cb, bank, N):
        wt = load_w_cb(cb)
        for kc in range(KC):
            mm(bank[:, 0:N], wt[:, kc * 128:(kc + 1) * 128], uT[:, kc, 0:N], [bank], [wt, uT],
               start=(kc == 0), stop=(kc == KC - 1))

    def shift_mix(bank, N, mucol, carry, out):
        cp("act", pb[:, 1:N + 1], bank[:, 0:N], [pb], [bank])
        cp("pool", pb[:, 0:1], carry, [pb], [car])
        ts("dve", tmp[:, 0:N], pb[:, 0:N], pv[:, mucol:mucol + 1], ALU.mult, [tmp], [pb, pv])
        stt(out[:, 0:N], pb[:, 1:N + 1], om[:, mucol:mucol + 1], tmp[:, 0:N], ALU.mult, ALU.add, [out], [pb, om, tmp])
        cp("pool", carry, pb[:, N:N + 1], [car], [pb])

    def phase1_tile(ti):
        nch = 1 if ti == 0 else 4
        N = nch * 128
        if ti == 0:
            dma(uT[:, :, 0:128], u_meta.ap().rearrange("p (k t) -> p k t", t=128), [uT], ["u_meta"], "d_ld")
        else:
            rb = (ti - 1) * 128
            dma(uT[:].rearrange("p k t -> p (k t)"), u_loc[rb:rb + 128, :], [uT], ["u_loc"], "d_ld")
        proj(0, PJ[0], N); proj(1, PJ[1], N)
        shift_mix(PJ[0], N, 0, car[:, 0, 0:1], zwm)
        act(tw[:, 0:N], zwm[:, 0:N], AF.Tanh, [tw], [zwm])
        shift_mix(PJ[1], N, 1, car[:, 1, 0:1], zwm)
        cp("act", zab[:, 0:N], zwm[:, 0:N], [zab], [zwm])
        for hp in range(HPC):
            pb0 = 2 + NPAR * hp
            cb0 = 2 + 8 * hp
            cr = 2 + 4 * hp
            hs = slice(hp * 128, (hp + 1) * 128)
            for i in range(4):
                proj(cb0 + i, PJ[i], N)
            shift_mix(PJ[0], N, pb0 + 0, car[:, cr + 0, 0:1], rz)
            shift_mix(PJ[1], N, pb0 + 1, car[:, cr + 1, 0:1], kz)
            shift_mix(PJ[2], N, pb0 + 2, car[:, cr + 2, 0:1], vz)
            act(sgA[:, 0:N], PJ[3][:, 0:N], AF.Sigmoid, [sgA], [PJ[3]])
            mm(PJ[0][:, 0:N], w2b[:, hs], tw[:, 0:N], [PJ[0]], [w2b, tw])
            act(Sx[:, 0:N], PJ[0][:, 0:N], AF.Sigmoid, [Sx], [PJ[0], pv], bias=pv[:, pb0 + 3:pb0 + 4])
            mm(PJ[1][:, 0:N], a2b[:, hs], zab[:, 0:N], [PJ[1]], [a2b, zab])
            act(aa[:, 0:N], PJ[1][:, 0:N], AF.Sigmoid, [aa], [PJ[1], pv], bias=pv[:, pb0 + 4:pb0 + 5])
            for ch in range(nch):
                cs = slice(ch * 128, (ch + 1) * 128)
                P.emit("dve", lambda e, cs=cs: e.tensor_tensor_scan(Sc[:, cs], ones[:, cs], Sx[:, cs], 0.0, ALU.mult, ALU.add),
                       w=[Sc], r=[ones, Sx])
            tt("pool", Sx[:, 0:N], Sc[:, 0:N], Sx[:, 0:N], ALU.subtract, [Sx], [Sc, Sx])
            act(eP[:, 0:N], Sc[:, 0:N], AF.Exp, [eP], [Sc], scale=-CDEC)
            act(ePx[:, 0:N], Sx[:, 0:N], AF.Exp, [ePx], [Sx], scale=-CDEC)
            act(eN[:, 0:N], Sc[:, 0:N], AF.Exp, [eN], [Sc], scale=CDEC)
            act(eEnd[:, 0:nch], Sc[:, 0:N].rearrange("p (c t) -> p c t", t=128)[:, :, 127], AF.Exp, [eEnd], [Sc], scale=-CDEC)
            ts("dve", kkr[:, 0:N], kz[:, 0:N], pv[:, pb0 + 5:pb0 + 6], ALU.mult, [kkr], [kz, pv])
            tt("pool", sqb[:, 0:N], kkr[:, 0:N], kkr[:, 0:N], ALU.mult, [sqb], [kkr])
            mm(PJ[2][:, 0:N], blk1b, sqb[:, 0:N], [PJ[2]], [cstb, sqb])
            act(t2[:, 0:N], PJ[2][:, 0:N], AF.Sqrt, [t2], [PJ[2]])
            ts("dve", t2[:, 0:N], t2[:, 0:N], 1e-12, ALU.max, [t2], [t2])
            P.emit("dve", lambda e: e.reciprocal(t3[:, 0:N], t2[:, 0:N]), w=[t3], r=[t2])
            tt("dve", kk[:, 0:N], kkr[:, 0:N], t3[:, 0:N], ALU.mult, [kk], [kkr, t3])
            ts("dve", t2[:, 0:N], aa[:, 0:N], pv[:, pb0 + 6:pb0 + 7], ALU.mult, [t2], [aa, pv, om],
               s2=om[:, pb0 + 6:pb0 + 7], op1=ALU.add)
            tt("pool", k2[:, 0:N], kz[:, 0:N], t2[:, 0:N], ALU.mult, [k2], [kz, t2])
            stt(bn[:, 0:N], aa[:, 0:N], -1.0, kk[:, 0:N], ALU.mult, ALU.mult, [bn], [aa, kk])
            v3 = lambda a: a[:, 0:N].rearrange("p (c t) -> p c t", t=128)
            tt("dve", KR[:, 0:nch, 0, :], v3(kk), v3(ePx), ALU.mult, [KR], [kk, ePx])
            tt("pool", KR[:, 0:nch, 1, :], v3(rz), v3(eP), ALU.mult, [KR], [rz, eP])
            tt("dve", kt[:, 0:N], k2[:, 0:N], eN[:, 0:N], ALU.mult, [kt], [k2, eN])
            tt("pool", bnt[:, 0:N], bn[:, 0:N], eN[:, 0:N], ALU.mult, [bnt], [bn, eN])
            cp("act", vb[:, 0:N], vz[:, 0:N], [vb], [vz])
            for ch in range(nch):
                cs = slice(ch * 128, (ch + 1) * 128)
                ts("dve", kh[:, cs], kt[:, cs], eEnd[:, ch:ch + 1], ALU.mult, [kh], [kt, eEnd])
                ts("pool", bh[:, cs], bnt[:, cs], eEnd[:, ch:ch + 1], ALU.mult, [bh], [bnt, eEnd])
            stt(t3[:, 0:N], rz[:, 0:N], pv[:, pb0 + 7:pb0 + 8], k2[:, 0:N], ALU.mult, ALU.mult, [t3], [rz, pv, k2])
            for e_ in range(2):
                mc = blk1[:, e_ * 64:e_ * 64 + 1]
                ts("pool", KRm[:, 0:nch, e_, :], KR[:, 0:nch, 0, :], mc, ALU.mult, [KRm], [KR, cst])
                ts("dve", bntm[:, e_, 0:N], bnt[:, 0:N], mc, ALU.mult, [bntm], [bnt, cst])
                ts("pool", ktm[:, e_, 0:N], kt[:, 0:N], mc, ALU.mult, [ktm], [kt, cst])
            for ch in range(nch):
                if SKIP_SCAN:
                    break
                cs = slice(ch * 128, (ch + 1) * 128)
                for i, src in enumerate([KR[:, ch, 0, :], vb[:, cs], kh[:, cs], bh[:, cs]]):
                    trp(pstr[:, i * 128:(i + 1) * 128], src, identb, [pstr], [src, cstb])
                cp("act", TM[:].rearrange("p a b -> p (a b)"), pstr[:, 0:512], [TM], [pstr])
                krc = KR[:, ch, :, :].rearrange("p a b -> p (a b)")
                for e_ in range(2):
                    mm(SC[:, e_ * 128:(e_ + 1) * 128], KRm[:, ch, e_, :], bnt[:, cs], [SC], [KRm, bnt])
                    mm(SA[:, e_ * 256:(e_ + 1) * 256], bntm[:, e_, cs], krc, [SA], [KR, bntm])
                    mm(SB_[:, e_ * 256:(e_ + 1) * 256], ktm[:, e_, cs], krc, [SB_], [KR, ktm])
                sa4 = SA[:, :].rearrange("p (e s t) -> p e s t", e=2, s=2)
                sc3 = SC[:, 0:256].rearrange("p (e t) -> p e t", e=2)
                for e_ in range(2):
                    tt("dve", N_[0][:, e_, :], sc3[:, e_, :], m_sl, ALU.mult, [N_[0]], [SC, cstb])
                    tt("dve", Q_[0][:, e_, 0, :], sa4[:, e_, 0, :], m_su, ALU.mult, [Q_[0]], [SA, cstb])
                    tt("dve", ArT[:, e_, :], sa4[:, e_, 1, :], m_ui, ALU.mult, [ArT], [SA, cstb])
                    tt("dve", BB[:, e_, :, :].rearrange("p a b -> p (a b)"), SB_[:, e_ * 256:(e_ + 1) * 256], m_suui, ALU.mult, [BB], [SB_, cstb])
                for e_ in range(2):
                    tt("pool", Q_[1][:, e_, 1, :], Q_[0][:, e_, 0, :], identb, ALU.add, [Q_[1]], [Q_[0], cstb])
                for e_ in range(2):
                    mm(SA[:, e_ * 256:e_ * 256 + 128], N_[0][:, e_, :], Q_[0][:, e_, 0, :], [SA], [N_[0], Q_[0]])
                    mm(SC[:, e_ * 128:(e_ + 1) * 128], Q_[0][:, e_, 0, :], N_[0][:, e_, :], [SC], [N_[0], Q_[0]])
                cp("act", Q_[1][:, :, 0, :], sa4[:, :, 0, :], [Q_[1]], [SA])
                cp("dve", N_[1][:], sc3, [N_[1]], [SC])
                cur = 1
                for lev in range(1, 7):
                    nxt = 1 - cur
                    last = lev == 6
                    for e_ in range(2):
                        mm(SA[:, e_ * 256:(e_ + 1) * 256], N_[cur][:, e_, :], Q_[cur][:, e_, :, :].rearrange("p a b -> p (a b)"),
                           [SA], [N_[cur], Q_[cur]])
                        if not last:
                            mm(SC[:, e_ * 128:(e_ + 1) * 128], Q_[cur][:, e_, 0, :], N_[cur][:, e_, :], [SC], [N_[cur], Q_[cur]])
                    tt("dve", Q_[nxt][:, :, 1, :], sa4[:, :, 1, :], Q_[cur][:, :, 1, :], ALU.add, [Q_[nxt]], [SA, Q_[cur]])
                    if not last:
                        cp("act", Q_[nxt][:, :, 0, :], sa4[:, :, 0, :], [Q_[nxt]], [SA])
                        cp("act", N_[nxt][:], sc3, [N_[nxt]], [SC])
                    cur = nxt
                TTf = Q_[cur]
                cp("pool", XK[:, :, 0:64], TM[:, 0, :].rearrange("p (e j) -> p e j", e=2), [XK], [TM])
                for e_ in range(2):
                    mm(SC[:, 256 + e_ * 64:256 + (e_ + 1) * 64], BB[:, e_, 0, :], TM[:, 1, e_ * 64:(e_ + 1) * 64], [SC], [BB, TM])
                cp("act", XK[:, :, 64:128], SC[:, 256:384].rearrange("p (e j) -> p e j", e=2), [XK], [SC])
                for e_ in range(2):
                    mm(SB_[:, e_ * 128:(e_ + 1) * 128], TTf[:, e_, 1, :], XK[:, e_, :], [SB_], [TTf, XK])
                sb3 = SB_[:, 0:256].rearrange("p (e x) -> p e x", e=2)
                cp("act", WZ[:, 0, :].rearrange("p (e j) -> p e j", e=2), sb3[:, :, 0:64], [WZ], [SB_])
                cp("dve", WZ[:, 1, :].rearrange("p (e j) -> p e j", e=2), sb3[:, :, 64:128], [WZ], [SB_])
                Wall = WZ[:, 0, :]
                Zall = WZ[:, 1, :]
                mm(SA[:, 0:128], Wall, TM[:, 3, :], [SA], [WZ, TM])
                mm(SA[:, 128:256], TM[:, 3, :], Zall, [SA], [WZ, TM], start=True, stop=False)
                mm(SA[:, 128:256], TM[:, 2, :], TM[:, 1, :], [SA], [TM], start=False, stop=True)
                tt("dve", t4[:, 0:128], SA[:, 0:128], blk1, ALU.mult, [t4], [SA, cst])
                stt(GT[:], ident, eEnd[:, ch:ch + 1], t4[:, 0:128], ALU.mult, ALU.add, [GT], [cst, eEnd, t4])
                tt("dve", Hs[:], SA[:, 128:256], blk1, ALU.mult, [Hs], [SA, cst])
                for e_ in range(2):
                    mm(SB_[:, 256 + e_ * 128:256 + (e_ + 1) * 128], Wall, ArT[:, e_, :], [SB_], [WZ, ArT])
                for e_ in range(2):
                    pr = slice(e_ * 64, (e_ + 1) * 64)
                    tt("dve", RpT[pr, :], SB_[pr, 256 + e_ * 128:256 + (e_ + 1) * 128], KR[pr, ch, 1, :], ALU.add, [RpT], [SB_, KR])
                for e_ in range(2):
                    reg = SC[:, e_ * 128:(e_ + 1) * 128]
                    mm(reg, TM[:, 1, :], BB[:, e_, 1, :], [SC], [TM, BB], start=True, stop=False)
                    mm(reg, Zall, ArT[:, e_, :], [SC], [WZ, ArT], start=False, stop=False)
                    mm(reg, Mb[:, hp, :], RpT[:, :], [SC], [Mb, RpT], start=False, stop=True)
                for e_ in range(2):
                    pr = slice(e_ * 64, (e_ + 1) * 64)
                    cp("act", yy[pr, cs], SC[pr, e_ * 128:(e_ + 1) * 128], [yy], [SC])
                mm(SA[:, 256:384], GT[:], Mb[:, hp, :], [SA], [GT, Mb])
                tt("dve", M32[:, hp, :], SA[:, 256:384], Hs[:], ALU.add, [M32], [SA, Hs])
                cp("act", Mb[:, hp, :], M32[:, hp, :], [Mb], [M32])
            if SKIP_SCAN:
                P.emit("pool", lambda e: e.memset(yy[:], 0.0), w=[yy])
            mm(PJ[0][:, 0:N], blka, yy[:, 0:N], [PJ[0]], [cst, yy])
            tt("dve", yc[:, 0:N], yy[:, 0:N], PJ[0][:, 0:N], ALU.subtract, [yc], [yy, PJ[0]])
            tt("pool", t2[:, 0:N], yc[:, 0:N], yc[:, 0:N], ALU.mult, [t2], [yc])
            mm(PJ[1][:, 0:N], blka, t2[:, 0:N], [PJ[1]], [cst, t2])
            act(t2[:, 0:N], PJ[1][:, 0:N], AF.Sqrt, [t2], [PJ[1], eps2], bias=eps2[:, 0:1])
            P.emit("dve", lambda e: e.reciprocal(kkr[:, 0:N], t2[:, 0:N]), w=[kkr], r=[t2])
            tt("dve", yc[:, 0:N], yc[:, 0:N], kkr[:, 0:N], ALU.mult, [yc], [yc, kkr])
            ts("dve", yc[:, 0:N], yc[:, 0:N], pv[:, pb0 + 8:pb0 + 9], ALU.mult, [yc], [yc, pv],
               s2=pv[:, pb0 + 9:pb0 + 10], op1=ALU.add)
            mm(PJ[2][:, 0:N], blk1, t3[:, 0:N], [PJ[2]], [cst, t3])
            tt("dve", t2[:, 0:N], vz[:, 0:N], PJ[2][:, 0:N], ALU.mult, [t2], [vz, PJ[2]])
            tt("pool", yc[:, 0:N], yc[:, 0:N], t2[:, 0:N], ALU.add, [yc], [yc, t2])
            tt("dve", mA[:, 0:N], yc[:, 0:N], sgA[:, 0:N], ALU.mult, [mA], [yc, sgA])
            for i in range(4):
                proj(cb0 + 4 + i, PJ[i], N)
            cp("act", hb[:, 0:N], PJ[2][:, 0:N], [hb], [PJ[2]])
            cp("pool", ub[:, 0:2], car[:, cr + 3, 0:2], [ub], [car])
            tt("dve", ub[:, 2:N + 2], hb[:, 0:N], PJ[1][:, 0:N], ALU.mult, [ub], [hb, PJ[1]])
            ts("dve", acc[:, 0:N], ub[:, 0:N], pv[:, pb0 + 10:pb0 + 11], ALU.mult, [acc], [ub, pv])
            stt(acc[:, 0:N], ub[:, 1:N + 1], pv[:, pb0 + 11:pb0 + 12], acc[:, 0:N], ALU.mult, ALU.add, [acc], [ub, pv, acc])
            stt(acc[:, 0:N], ub[:, 2:N + 2], pv[:, pb0 + 12:pb0 + 13], acc[:, 0:N], ALU.mult, ALU.add, [acc], [ub, pv, acc])
            cp("pool", car[:, cr + 3, 0:2], ub[:, N:N + 2], [car], [ub])
            tt("dve", acc[:, 0:N], acc[:, 0:N], PJ[0][:, 0:N], ALU.mult, [acc], [acc, PJ[0]])
            act(hb[:, 0:N], PJ[3][:, 0:N], AF.Sigmoid, [hb], [PJ[3]])
            tt("pool", mB[:, 0:N], acc[:, 0:N], hb[:, 0:N], ALU.mult, [mB], [acc, hb])
            if ti > 0:
                q, off = divmod((ti - 1) * 512, QT)
                r0 = (q * c.NT2 + off // 512) * 128
                dma(m_loc[r0:r0 + 128, (2 * hp) * 512:(2 * hp + 1) * 512], mA[:, 0:512], ["m_loc"], [mA], "d_st")
                dma(m_loc[r0:r0 + 128, (2 * hp + 1) * 512:(2 * hp + 2) * 512], mB[:, 0:512], ["m_loc"], [mB], "d_st")

    for ti in range(c.NT1 + 1):
        if stop_after == 1.0:
            break
        if stop_after == 1.2 and ti == 1:
            break
        if stop_after in (1.4, 1.5) and ti == 2:
            break
        phase1_tile(ti)
        if ti == 1 and stop_after != 1.4:
            import os
            _m = os.environ.get("DBG_AG", "ouw")
            if "o" in _m: gather_chunked(wo_loc, wo_all, c.NBO * 128, c.KOl * 512)
            if "u" in _m: gather_chunked(wu_loc, wu_all, c.HPR * c.NSP * c.NKH * 128, c.PKu * c.SPW)
            if "w" in _m: gather_chunked(wd_loc, wd_all, c.HPR * c.NBO * 128, c.FRC * 512)
    if stop_after not in (1.0, 1.2, 1.3, 1.4, 1.5):
        gather_chunked(m_loc, m_all, 4 * c.NT2 * 128, 2 * HPC * 512)
    P.finish_phase()
    P.replay(nc, sems)
    s1.close()
    if 1 <= stop_after < 2:
        es.close()
        return nc

    s2 = ExitStack()
    h = sb(s2, "h", [128, 4, D])
    RB = sb(s2, "RB", [128, c.KO * 512], BF16)
    mT = RB[:, :].rearrange("p (k t) -> p k t", t=512)
    uT2 = RB[:, 0:KC * 512].rearrange("p (k t) -> p k t", t=512)
    hidT = RB[:, KC * 512:(KC + c.FRC) * 512].rearrange("p (k t) -> p k t", t=512)
    junk2 = RB[:, (KC + c.FRC) * 512:(KC + c.FRC) * 512 + D]
    assert (KC + c.FRC) * 512 + D <= c.KO * 512
    ALIAS = ["mT", "uT2", "hidT", "junk2"]
    fsc = sb(s2, "fsc", [128, 1])
    def fence():
        P.emit("dve", lambda e: e.memset(fsc[:], 0.0), w=ALIAS + [fsc])
    rtmp = sb(s2, "rtmp", [128, 512])
    gfb = sb(s2, "gfb", [128, 1024])
    ss2 = sb(s2, "ss2", [128, 1]); rt2 = sb(s2, "rt2", [128, 1]); rstd2 = sb(s2, "rstd2", [128, 4])
    diag2 = sb(s2, "diag2", [128, 128])
    wp = [sb(s2, f"wp{i}", [128, c.WS], BF16) for i in range(2)]
    wpi = [0]
    qcache = {}

    def wslot():
        i = wpi[0] % 2
        wpi[0] += 1
        return wp[i], f"d_w{i}"

    def phase2_tile(tt_):
        t0 = tt_ * 512
        if stop_after in (2.0, 2.01):
            return
        for s in range(4):
            dma(h[:, s, :], x_d[t0 + s * 128:t0 + (s + 1) * 128, :], [h], [], "d_x")
        if stop_after == 2.05:
            return

        ld_eng = "sp" if tt_ < 2 else "act"

        Rm, Cm = 4 * c.NT2 * 128, 2 * HPC * 512
        rcm = rcof(Rm, Cm)
        nseg = max(1, 128 // rcm)
        assert rcm >= 128 and rcm % 128 == 0 or 128 % rcm == 0

        def ld_m(e):
            if ld_eng not in qcache:
                qcache[ld_eng] = (e.partition_id() % 4)
            qq = qcache[ld_eng]
            last = None
            if rcm >= 128:
                bpc = rcm // 128
                assert c.NT2 % bpc == 0 or bpc % c.NT2 == 0
                for g in range(4):
                    if bpc >= 4 * c.NT2:
                        base = g * rcm + tt_ * 128
                        dyn = qq * (c.NT2 * 128)
                        span = 3 * c.NT2 * 128
                    else:
                        assert c.NT2 % bpc == 0
                        base = (tt_ // bpc) * 4 * rcm + g * rcm + (tt_ % bpc) * 128
                        dyn = qq * ((c.NT2 // bpc) * 4 * rcm)
                        span = 3 * (c.NT2 // bpc) * 4 * rcm
                    win = m_all.ap()[base:base + span + 128, :]
                    if last is not None:
                        last.then_inc(sems["d_ld"], 16)
                    last = e.dma_start(out=RB[:, g * Cm:(g + 1) * Cm], in_=win[bass.ds(dyn, 128), :])
                return last, 4
            mv = m_all.ap().rearrange("(k g r) c -> k g r c", g=4, r=rcm)
            for h_ in range(nseg):
                k0 = tt_ * nseg + h_
                win = mv[k0:k0 + 3 * c.NT2 * nseg + 1]
                src = win[bass.ds(qq * (c.NT2 * nseg), 1)].rearrange("k g r c -> (k r) g c")
                if last is not None:
                    last.then_inc(sems["d_ld"], 16)
                last = e.dma_start(out=RB[h_ * rcm:(h_ + 1) * rcm, :].rearrange("p (g c) -> p g c", g=4), in_=src)
            return last, nseg

        ndm = 4 if rcm >= 128 else nseg
        P.emit(ld_eng, lambda e: ld_m(e)[0], w=["mT"], r=["m_all"], sig="d_ld", inc=16)
        P.cnt["d_ld"] += 16 * (ndm - 1)
        if stop_after == 2.1:
            return
        for n in range(c.NBO):
            for kq in range(c.NKQ):
                wt, sg = wslot()
                for rr in range(c.RPP):
                    rk = kq * c.RPP + rr
                    gload(wt, rr * c.KOl * 512, c.KOl * 512, wo_all, rk, n * 128, c.NBO * 128, c.KOl * 512, sg)
                for s in range(4):
                    for k in range(c.PK):
                        kcg = kq * c.PK + k
                        mm(PJ[s][:, :], mT[:, kcg, s * 128:(s + 1) * 128], wt[:, k * 512:(k + 1) * 512], [PJ[s]], ["mT", wt],
                           start=(kq == 0 and k == 0), stop=(kq == c.NKQ - 1 and k == c.PK - 1))
            for s in range(4):
                tt("dve", h[:, s, n * 512:(n + 1) * 512], h[:, s, n * 512:(n + 1) * 512], PJ[s][:, :], ALU.add, [h], [h, PJ[s]])
        if stop_after == 2.2:
            return
        fence()
        for s in range(4):
            rms_to_uT(h[:, s, :], GMLP, uT2, s * 128, junk2, ss2[:], rt2[:], rstd2[:, s:s + 1], diag2[:], uTk="uT2", junkk="junk2")
        nhc = c.SPW // 128
        for j in range(8):
            for sp_ in range(c.NSP):
                for kh_ in range(c.NKH):
                    wt, sg = wslot()
                    gload(wt, 0, c.PKu * c.SPW, wu_all, j // c.HPR, (((j % c.HPR) * c.NSP + sp_) * c.NKH + kh_) * 128,
                          c.HPR * c.NSP * c.NKH * 128, c.PKu * c.SPW, sg)
                    for hc in range(nhc):
                        for k in range(c.PKu):
                            kc = kh_ * c.PKu + k
                            mm(PJ[hc][:, :], wt[:, k * c.SPW + hc * 128:k * c.SPW + (hc + 1) * 128], uT2[:, kc, :], [PJ[hc]], [wt, "uT2"],
                               start=(kh_ == 0 and k == 0), stop=(kh_ == c.NKH - 1 and k == c.PKu - 1))
                for hc in range(nhc):
                    act(rtmp[:], PJ[hc][:, :], AF.Relu, [rtmp], [PJ[hc]])
                    tt("pool", hidT[:, sp_ * nhc + hc, :], rtmp[:], rtmp[:], ALU.mult, ["hidT"], [rtmp])
            for n in range(c.NBO):
                wt, sg = wslot()
                gload(wt, 0, c.FRC * 512, wd_all, j // c.HPR, ((j % c.HPR) * c.NBO + n) * 128, c.HPR * c.NBO * 128, c.FRC * 512, sg)
                for s in range(4):
                    for k in range(c.FRC):
                        mm(PJ[s][:, :], hidT[:, k, s * 128:(s + 1) * 128], wt[:, k * 512:(k + 1) * 512], [PJ[s]], ["hidT", wt],
                           start=(k == 0), stop=(k == c.FRC - 1))
                for s in range(4):
                    tt("dve", h[:, s, n * 512:(n + 1) * 512], h[:, s, n * 512:(n + 1) * 512], PJ[s][:, :], ALU.add, [h], [h, PJ[s]])
        if stop_after == 2.3:
            return
        for s in range(4):
            act(junk2[:], h[:, s, :], AF.Square, ["junk2", ss2], [h], accum=ss2[:])
            act(rt2[:], ss2[:], AF.Sqrt, [rt2], [ss2, eps1], bias=eps1[:, 0:1], scale=1.0 / D)
            P.emit("dve", lambda e, s=s: e.reciprocal(rstd2[:, s:s + 1], rt2[:]), w=[rstd2], r=[rt2])
        for cq in range(D // 1024 if D >= 1024 else 1):
            cw = min(1024, D)
            dma(gfb[:, 0:cw], gf_d[:, cq * cw:(cq + 1) * cw].partition_broadcast(128), [gfb], [], "d_ld")
            for s in range(4):
                stt(h[:, s, cq * cw:(cq + 1) * cw], h[:, s, cq * cw:(cq + 1) * cw], rstd2[:, s:s + 1], gfb[:, 0:cw],
                    ALU.mult, ALU.mult, [h], [h, rstd2, gfb])
        fence()
        for s in range(4):
            dma(out_d[t0 + s * 128:t0 + (s + 1) * 128, :], h[:, s, :], ["out"], [h], "d_out")

    for tt_ in range(c.NT2):
        phase2_tile(tt_)
    P.finish_phase()
    if stop_after != 2.01:
        P.replay(nc, sems)
    s2.close()
    es.close()
    return nc


def make_consts():
    cst = np.zeros((128, 1024), np.float32)
    cst[:, 0:128] = np.eye(128)
    p = np.arange(128)
    blk = (p[:, None] // 64 == p[None, :] // 64).astype(np.float32)
    cst[:, 128:256] = blk
    cst[:, 256:384] = blk / 64.0
    cst[:, 384:448] = (p[:, None] % 64 == np.arange(64)[None, :]).astype(np.float32)
    cst[:, 512:640] = (p[None, :] < p[:, None])
    cst[:, 640:768] = (p[None, :] > p[:, None])
    cst[:, 768:896] = (p[None, :] >= p[:, None])
    return cst


def prep_inputs(cfg, inp):
    c = cfg
    D, KC, HPC = c.D, c.KC, c.HPC
    f = lambda a: np.ascontiguousarray(np.asarray(a, dtype=np.float32))
    x = f(inp["x"]); w_in = f(inp["w_in"])[0]; w_out = f(inp["w_out"])[0]
    w_up = f(inp["w_up"])[0]; w_down = f(inp["w_down"])[0]
    mu = f(inp["rwkv_shift_mu"])[0]
    w0 = f(inp["rwkv_w0"])[0]; a0 = f(inp["rwkv_a0"])[0]; k_k = f(inp["rwkv_k_k"])[0]; k_a = f(inp["rwkv_k_a"])[0]
    r_k = f(inp["rwkv_r_k"])[0].reshape(-1); ln_w = f(inp["rwkv_ln_w"])[0]; ln_b = f(inp["rwkv_ln_b"])[0]
    cw = f(inp["conv_w"])[0]
    w2 = f(inp["rwkv_w2"])[0]; a2 = f(inp["rwkv_a2"])[0]
    g_mix = f(inp["norm_mix_g"])[0]; g_mlp = f(inp["norm_mlp_g"])[0]; gf = f(inp["norm_final_g"]).reshape(1, D)
    meta = f(inp["meta_tokens"])
    NR = c.NR
    cst = make_consts()
    perm = []
    for g in range(4):
        for hp in range(HPC):
            gp = g * HPC + hp
            perm.append(np.arange(gp * 128, (gp + 1) * 128))
            perm.append(D + np.arange(gp * 128, (gp + 1) * 128))
    perm = np.concatenate(perm)
    w_out_p = w_out[perm]
    maps = []
    for core in range(8):
        b, g = divmod(core, 4)
        q = g
        cols = [np.arange(3 * D, 3 * D + 128), np.arange(3 * D + 128, 3 * D + 256)]
        pvc = [mu[3 * D:3 * D + 128], mu[3 * D + 128:3 * D + 256]]
        for hp in range(HPC):
            gp = g * HPC + hp
            ch = np.arange(gp * 128, (gp + 1) * 128)
            cols += [ch, D + ch, 2 * D + ch, NR + 3 * D + ch, NR + ch, NR + D + ch, NR + 2 * D + ch, NR + 4 * D + ch]
            pvc += [mu[ch], mu[D + ch], mu[2 * D + ch], w0[ch], a0[ch], k_k[ch], k_a[ch], r_k[ch], ln_w[ch], ln_b[ch],
                    cw[0][ch], cw[1][ch], cw[2][ch]]
        pvm = np.stack(pvc, axis=1)
        pvm = np.concatenate([pvm, g_mix.reshape(KC, 128).T, g_mlp.reshape(KC, 128).T], axis=1)
        cols = np.concatenate(cols)
        ws = w_in[:, cols]
        win = ws.reshape(KC, 128, c.NCB, 128).transpose(2, 1, 0, 3).reshape(c.NCB * 128, KC * 128)
        chs = np.arange(g * HPC * 128, (g + 1) * HPC * 128)
        r = g
        wo_r = w_out_p[r * c.KOl * 128:(r + 1) * c.KOl * 128]
        wo = wo_r.reshape(c.KOl, 128, c.NBO, 512).transpose(2, 1, 0, 3).reshape(c.NBO * 128, c.KOl * 512)
        wu_r = w_up[:, r * c.HPR * c.FR:(r + 1) * c.HPR * c.FR]
        wu = wu_r.reshape(c.NKH, c.PKu, 128, c.HPR, c.NSP, c.SPW).transpose(3, 4, 0, 2, 1, 5).reshape(c.HPR * c.NSP * c.NKH * 128, c.PKu * c.SPW)
        wd_r = w_down[r * c.HPR * c.FR:(r + 1) * c.HPR * c.FR]
        wd = wd_r.reshape(c.HPR, c.FRC, 128, c.NBO, 512).transpose(0, 3, 2, 1, 4).reshape(c.HPR * c.NBO * 128, c.FRC * 512)
        maps.append({
            "x": np.ascontiguousarray(x[b, q * c.QT:(q + 1) * c.QT]), "xb": np.ascontiguousarray(x[b]),
            "meta": meta, "win": np.ascontiguousarray(win), "wo": np.ascontiguousarray(wo),
            "wu": np.ascontiguousarray(wu), "wd": np.ascontiguousarray(wd),
            "pv": np.ascontiguousarray(pvm), "w2": np.ascontiguousarray(w2[:, chs]), "a2": np.ascontiguousarray(a2[:, chs]),
            "gf": gf, "cst": cst,
        })
    return maps


def run_cfg(cfg, inp, stop_after=9, skip_scan=False):
    nc = build(cfg, stop_after, skip_scan)
    maps = prep_inputs(cfg, inp)
    res = run_bass_kernel_spmd(nc, maps, core_ids=list(range(8)))
    out = np.zeros((2, cfg.SEQ, cfg.D), np.float32)
    for core in range(8):
        b, q = divmod(core, 4)
        out[b, q * cfg.QT:(q + 1) * cfg.QT] = res.results[core]["out"]
    return out


def kernel(**inputs):
    return run_cfg(FULL, inputs)
```

```python
import numpy as np
import ml_dtypes
import concourse.bass as bass
import concourse.mybir as mybir
from concourse.bass_utils import run_bass_kernel_spmd

F32 = mybir.dt.float32
BF16 = mybir.dt.bfloat16
AF = mybir.ActivationFunctionType
ALU = mybir.AluOpType
NPAR = 13
CDEC = 0.6065306597126334


class Cfg:
    def __init__(s, D, SEQ, PK):
        s.D = D; s.SEQ = SEQ; s.KC = D // 128; s.NH = D // 64; s.NP = s.NH // 2; s.HPC = s.NP // 4
        s.DFF = 4 * D; s.FR = s.DFF // 8; s.FRC = s.FR // 128
        s.QT = SEQ // 4; s.NT2 = s.QT // 512; s.NT1 = SEQ // 512
        s.NCB = 2 + 8 * s.HPC; s.MC = 2 * s.HPC * 128
        s.KO = 2 * s.KC; s.KOl = s.KO // 4; s.HPR = 2; s.CH = 524288; s.PK = PK; s.RPP = PK // s.KOl; s.NKQ = s.KO // PK
        s.PKu = min(PK, s.KC); s.NKH = s.KC // s.PKu
        s.SPW = min(512, s.FR); s.NSP = s.FR // s.SPW
        s.NBO = D // 512
        s.NR = 3 * D + 256
        s.NPV = 2 + NPAR * s.HPC + 2 * s.KC
        s.WS = max(PK * 512, s.PKu * s.SPW, s.FRC * 512, s.KC * 128)


FULL = Cfg(4096, 8192, 16)


class Prog:
    ENGS = ["pe", "act", "dve", "pool", "sp"]

    def __init__(self):
        self.streams = {e: [] for e in self.ENGS}
        self.cnt = {}
        self.lastw = {}
        self.readers = {}
        self.waited = {e: {} for e in self.ENGS}

    @staticmethod
    def key(k):
        if isinstance(k, str):
            return k
        if hasattr(k, "tensor"):
            return k.tensor.name
        return k.name

    def emit(self, eng, fn, w=(), r=(), sig=None, inc=1):
        sig = sig or ("e_" + eng)
        deps = {}

        def add(d):
            if d is not None:
                s, v = d
                if s.startswith("d_") or s.startswith("cc"):
                    v = self.cnt[s]
                deps[s] = max(deps.get(s, 0), v)

        rk = [self.key(k) for k in r]
        wk = [self.key(k) for k in w]
        for k in rk:
            add(self.lastw.get(k))
        for k in wk:
            add(self.lastw.get(k))
            for s, v in self.readers.get(k, {}).items():
                add((s, v))
        waits = []
        for s, v in deps.items():
            if s == "e_pe" and eng == "pe":
                continue
            if self.waited[eng].get(s, 0) < v:
                waits.append((s, v))
                self.waited[eng][s] = v
        self.cnt[sig] = self.cnt.get(sig, 0) + inc
        val = self.cnt[sig]
        self.streams[eng].append((waits, fn, sig, inc))
        for k in rk:
            d = self.readers.setdefault(k, {})
            d[sig] = max(d.get(sig, 0), val)
        for k in wk:
            self.lastw[k] = (sig, val)
            self.readers[k] = {}
        return val

    def wait_all(self, eng, sig):
        v = self.cnt.get(sig, 0)
        if self.waited[eng].get(sig, 0) < v:
            self.waited[eng][sig] = v
            self.streams[eng].append(([(sig, v)], None, None, 0))

    def finish_phase(self):
        waits = [(s, v) for s, v in self.cnt.items() if self.waited["sp"].get(s, 0) < v and s != "e_sp"]
        for s, v in waits:
            self.waited["sp"][s] = v
        self.streams["sp"].append((waits, None, None, 0))

    def replay(self, nc, sems):
        streams = self.streams
        self.streams = {e: [] for e in self.ENGS}

        def run(e, items):
            for waits, fn, sig, inc in items:
                for s, v in waits:
                    e.wait_ge(sems[s], v)
                if fn is not None:
                    ins = fn(e)
                    ins.then_inc(sems[sig], inc)

        with nc.Block() as block:
            @block.tensor
            def _(e):
                run(e, streams["pe"])

            @block.scalar
            def _(e):
                run(e, streams["act"])

            @block.vector
            def _(e):
                run(e, streams["dve"])

            @block.gpsimd
            def _(e):
                run(e, streams["pool"])

            @block.sync
            def _(e):
                run(e, streams["sp"])


SEM_NAMES = ["e_pe", "e_act", "e_dve", "e_pool", "e_sp", "d_ld", "d_st", "d_w0", "d_w1", "d_w2", "d_w3", "d_w4", "d_w5", "d_x", "d_cast",
             "cc", "d_out", "d_u"]


def build(cfg, stop_after=9, SKIP_SCAN=False):
    c = cfg
    D, KC, HPC, QT, SEQ = c.D, c.KC, c.HPC, c.QT, c.SEQ
    nc = bass.Bass("TRN2", target_bir_lowering=False)
    P = Prog()

    def din(name, shape, dt=F32):
        return nc.dram_tensor(name, list(shape), dt, kind="ExternalInput").ap()

    x_d = din("x", [QT, D]); xb_d = din("xb", [SEQ, D]); meta_d = din("meta", [16, D])
    win_d = din("win", [c.NCB * 128, KC * 128])
    wo_d = din("wo", [c.NBO * 128, c.KOl * 512])
    wu_d = din("wu", [c.HPR * c.NSP * c.NKH * 128, c.PKu * c.SPW])
    wd_d = din("wd", [c.HPR * c.NBO * 128, c.FRC * 512])
    pv_d = din("pv", [128, c.NPV]); w2_d = din("w2", [128, HPC * 128]); a2_d = din("a2", [128, HPC * 128])
    gf_d = din("gf", [1, D]); cst_d = din("cst", [128, 1024])
    out_d = nc.dram_tensor("out", [QT, D], F32, kind="ExternalOutput").ap()

    def dint(name, shape, dt=BF16):
        return nc.dram_tensor(name, list(shape), dt)

    winb = dint("winb", [c.NCB * 128, KC * 128])
    u_loc = dint("u_loc", [(SEQ // 512) * 128, KC * 512])
    u_meta = dint("u_meta", [128, KC * 128])
    wo_loc = dint("wo_loc", [c.NBO * 128, c.KOl * 512]); wo_all = dint("wo_all", [4 * c.NBO * 128, c.KOl * 512])
    wu_loc = dint("wu_loc", [c.HPR * c.NSP * c.NKH * 128, c.PKu * c.SPW])
    wu_all = dint("wu_all", [8 * c.NSP * c.NKH * 128, c.PKu * c.SPW])
    wd_loc = dint("wd_loc", [c.HPR * c.NBO * 128, c.FRC * 512]); wd_all = dint("wd_all", [8 * c.NBO * 128, c.FRC * 512])
    m_loc = dint("m_loc", [4 * c.NT2 * 128, 2 * HPC * 512]); m_all = dint("m_all", [16 * c.NT2 * 128, 2 * HPC * 512])

    from contextlib import ExitStack
    es = ExitStack()
    sems = {n: es.enter_context(nc.semaphore(n)) for n in SEM_NAMES}

    def sb(stack, name, shape, dt=F32):
        return stack.enter_context(nc.sbuf_tensor("s_" + name, list(shape), dt))

    def ps(stack, name, shape, dt=F32):
        return stack.enter_context(nc.psum_tensor(name, list(shape), dt))

    def mm(out, lhsT, rhs, w, r, start=True, stop=True):
        P.emit("pe", lambda e: e.matmul(out, lhsT=lhsT, rhs=rhs, start=start, stop=stop), w=w, r=r)

    def trp(out, in_, ident, w, r):
        P.emit("pe", lambda e: e.transpose(out, in_, ident), w=w, r=r)

    def act(out, in_, func, w, r, bias=None, scale=None, accum=None):
        kw = {}
        if bias is not None: kw["bias"] = bias
        if scale is not None: kw["scale"] = scale
        if accum is not None: kw["accum_out"] = accum
        P.emit("act", lambda e: e.activation(out, in_, func, **kw), w=w, r=r)

    def tt(eng, out, in0, in1, op, w, r):
        P.emit(eng, lambda e: e.tensor_tensor(out, in0, in1, op), w=w, r=r)

    def ts(eng, out, in0, s1, op0, w, r, s2=None, op1=None):
        if op1 is None:
            P.emit(eng, lambda e: e.tensor_scalar(out, in0, s1, None, op0), w=w, r=r)
        else:
            P.emit(eng, lambda e: e.tensor_scalar(out, in0, s1, s2, op0, op1), w=w, r=r)

    def stt(out, in0, scalar, in1, op0, op1, w, r):
        P.emit("dve", lambda e: e.scalar_tensor_tensor(out, in0, scalar, in1, op0, op1), w=w, r=r)

    def cp(eng, out, in_, w, r):
        if eng == "act":
            P.emit("act", lambda e: e.copy(out, in_), w=w, r=r)
        else:
            P.emit(eng, lambda e: e.tensor_copy(out, in_), w=w, r=r)

    def dma(out, in_, w, r, sig, eng="sp"):
        P.emit(eng, lambda e: e.dma_start(out=out, in_=in_), w=w, r=r, sig=sig, inc=16)

    def allgather(in_t, out_t, groups):
        P.emit("pool", lambda e: e.collective_compute("AllGather", ALU.bypass, replica_groups=groups,
                                                      ins=[in_t.ap().opt()], outs=[out_t.ap().opt()]),
               w=[out_t.name], r=[in_t.name], sig="cc", inc=1)

    G4 = [[0, 1, 2, 3], [4, 5, 6, 7]]

    def rcof(R, C):
        return min(R, max(1, c.CH // C))

    def gather_chunked(loc_t, all_t, R, C):
        rc = rcof(R, C)
        for k in range(R // rc):
            P.emit("pool", lambda e, k=k: e.collective_compute(
                "AllGather", ALU.bypass, replica_groups=G4,
                ins=[loc_t[k * rc:(k + 1) * rc, :].opt()], outs=[all_t[k * 4 * rc:(k + 1) * 4 * rc, :].opt()]),
                w=[all_t.name], r=[loc_t.name], sig="cc", inc=1)

    def gsegs(r, i0, n, R, C):
        rc = rcof(R, C)
        out = []
        i = i0
        while i < i0 + n:
            k, o = divmod(i, rc)
            nr = min(rc - o, i0 + n - i)
            out.append((i - i0, k * 4 * rc + r * rc + o, nr))
            i += nr
        return out

    def gload(dst_tile, col0, ncols, all_t, r, i0, R, C, sg):
        for po, src, nr in gsegs(r, i0, 128, R, C):
            dma(dst_tile[po:po + nr, col0:col0 + ncols], all_t[src:src + nr, :], [dst_tile], [all_t.name], sg)

    G8 = [list(range(8))]

    pv = sb(es, "pv", [128, c.NPV]); om = sb(es, "om", [128, c.NPV])
    cst = sb(es, "cst", [128, 1024])
    cstb = sb(es, "cstb", [128, 1024], BF16)
    eps1 = sb(es, "eps1", [128, 1]); eps2 = sb(es, "eps2", [128, 1])
    ident = cst[:, 0:128]
    blk1 = cst[:, 128:256]
    blka = cst[:, 256:384]
    idblk = cst[:, 384:448]
    identb = cstb[:, 0:128]
    blk1b = cstb[:, 128:256]
    m_sl = cstb[:, 512:640]
    m_su = cstb[:, 640:768]
    m_ui = cstb[:, 768:896]
    m_suui = cstb[:, 640:896]

    psb = [ps(es, f"psb{i}", [128, 512]) for i in range(7)]
    pstr = ps(es, "pstr", [128, 1024], BF16)
    PJ = psb[0:4]; SA = psb[4]; SB_ = psb[5]; SC = psb[6]

    dma(pv[:], pv_d, [pv], [], "d_ld")
    dma(cst[:], cst_d, [cst], [], "d_ld")
    cp("dve", cstb[:], cst[:], [cstb], [cst])
    ts("dve", om[:], pv[:], -1.0, ALU.mult, [om], [pv], s2=1.0, op1=ALU.add)
    P.emit("dve", lambda e: e.memset(eps1[:], 1e-6), w=[eps1])
    P.emit("dve", lambda e: e.memset(eps2[:], 64e-5), w=[eps2])

    GMIX = 2 + NPAR * HPC
    GMLP = GMIX + KC

    def rms_to_uT(xs, gcol0, uT, col0, junk, ss, rt, rstd, diag, uTk=None, junkk=None):
        uTk = uTk or uT
        junkk = junkk or junk
        act(junk, xs, AF.Square, [junkk, ss], [xs], accum=ss)
        act(rt, ss, AF.Sqrt, [rt], [ss, eps1], bias=eps1[:, 0:1], scale=1.0 / D)
        P.emit("dve", lambda e: e.reciprocal(rstd, rt), w=[rstd], r=[rt])
        ts("dve", diag, ident, rstd[:, 0:1], ALU.mult, [diag], [cst, rstd])
        for kc in range(KC):
            bank = PJ[kc % 4]
            reg = bank[:, 0:128]
            key = bank
            mm(reg, xs[:, kc * 128:(kc + 1) * 128], diag, [key], [xs, diag])
            act(uT[:, kc, col0:col0 + 128], reg, AF.Copy, [uTk], [key, pv],
                scale=pv[:, gcol0 + kc:gcol0 + kc + 1])
        return rstd

    s1 = ExitStack()
    NT = 512
    uT = sb(s1, "uT", [128, KC, NT], BF16)
    xs = sb(s1, "xs", [128, D])
    junk = sb(s1, "junk", [128, D], BF16)
    ss = sb(s1, "ss", [128, 1]); rt = sb(s1, "rt", [128, 1]); rstd = sb(s1, "rstd", [128, 1])
    diag = sb(s1, "diag", [128, 128])

    ncast = [0]

    def cast2d(dst_t, src_ap, rows, cols):
        for r0 in range(0, rows, 128):
            for c0 in range(0, cols, 2048):
                cw = min(2048, cols - c0)
                dma(dst_t[r0:r0 + 128, c0:c0 + cw], src_ap[r0:r0 + 128, c0:c0 + cw], [dst_t.name], [], "d_cast", eng="pool")
                ncast[0] += 1
                if ncast[0] % 16 == 0:
                    P.wait_all("pool", "d_cast")

    cast2d(winb, win_d, c.NCB * 128, KC * 128)
    cast2d(wo_loc, wo_d, c.NBO * 128, c.KOl * 512)
    cast2d(wu_loc, wu_d, c.HPR * c.NSP * c.NKH * 128, c.PKu * c.SPW)
    cast2d(wd_loc, wd_d, c.HPR * c.NBO * 128, c.FRC * 512)

    for t4 in range(SEQ // 512):
        for s in range(4):
            r0 = t4 * 512 + s * 128
            dma(xs[:], xb_d[r0:r0 + 128, :], [xs], [], "d_x")
            rms_to_uT(xs[:], GMIX, uT, s * 128, junk[:], ss[:], rt[:], rstd[:], diag[:])
        dma(u_loc[t4 * 128:(t4 + 1) * 128, :], uT[:].rearrange("p k t -> p (k t)"), ["u_loc"], [uT], "d_u")
    P.emit("pool", lambda e: e.memset(xs[:], 0.0), w=[xs])
    dma(xs[112:128, :], meta_d, [xs], [], "d_x")
    rms_to_uT(xs[:], GMIX, uT, 0, junk[:], ss[:], rt[:], rstd[:], diag[:])
    dma(u_meta.ap().rearrange("p (k t) -> p k t", t=128), uT[:, :, 0:128], ["u_meta"], [uT], "d_u")
    P.finish_phase()
    P.replay(nc, sems)
    s1.close()
    if stop_after == 0:
        es.close()
        return nc

    s1 = ExitStack()
    uT = sb(s1, "uT1", [128, KC, NT], BF16)
    wsl = [sb(s1, f"wsl{i}", [128, KC * 128], BF16) for i in range(6)]
    w2b = sb(s1, "w2b", [128, HPC * 128], BF16); a2b = sb(s1, "a2b", [128, HPC * 128], BF16)
    dma(w2b[:], w2_d, [w2b], [], "d_cast", eng="pool")
    dma(a2b[:], a2_d, [a2b], [], "d_cast", eng="pool")

    def f32t(name, n=NT + 2):
        return sb(s1, name, [128, n])

    def b16t(name, n=NT):
        return sb(s1, name, [128, n], BF16)

    pb = f32t("pb"); tmp = f32t("tmp"); rz = f32t("rz"); kz = f32t("kz"); vz = f32t("vz")
    zwm = f32t("zwm"); sgA = f32t("sgA"); Sc = f32t("Sc"); Sx = f32t("Sx"); aa = f32t("aa")
    kkr = f32t("kkr"); kk = f32t("kk"); k2 = f32t("k2"); bn = f32t("bn"); t2 = f32t("t2")
    eP = f32t("eP"); ePx = f32t("ePx"); eN = f32t("eN"); yy = f32t("yy"); yc = f32t("yc"); t3 = f32t("t3")
    ub = f32t("ub"); hb = f32t("hb"); acc = f32t("acc")
    eEnd = sb(s1, "eEnd", [128, 4])
    ones = f32t("ones")
    tw = b16t("tw"); zab = b16t("zab"); sqb = b16t("sqb"); vb = b16t("vb")
    KR = sb(s1, "KR", [128, 4, 2, 128], BF16)
    kt = b16t("kt"); bnt = b16t("bnt"); kh = b16t("kh"); bh = b16t("bh")
    mA = b16t("mA"); mB = b16t("mB")
    N_ = [sb(s1, f"Nn{i}", [128, 2, 128], BF16) for i in range(2)]
    Q_ = [sb(s1, f"Qq{i}", [128, 2, 2, 128], BF16) for i in range(2)]
    ArT = sb(s1, "ArT", [128, 2, 128], BF16)
    BB = sb(s1, "BB", [128, 2, 2, 128], BF16)
    TM = sb(s1, "TM", [128, 4, 128], BF16)
    XK = sb(s1, "XK", [128, 2, 128], BF16)
    WZ = sb(s1, "WZ", [128, 2, 128], BF16)
    GT = sb(s1, "GT", [128, 128], BF16)
    Hs = sb(s1, "Hs", [128, 128])
    t4 = f32t("t4")
    bon = f32t("bon")
    KRm = sb(s1, "KRm", [128, 4, 2, 128], BF16)
    bntm = sb(s1, "bntm", [128, 2, NT], BF16); ktm = sb(s1, "ktm", [128, 2, NT], BF16)
    RpT = sb(s1, "RpT", [128, 128], BF16)
    M32 = sb(s1, "M32", [128, HPC, 128]); Mb = sb(s1, "Mb", [128, HPC, 128], BF16)
    car = sb(s1, "car", [128, 2 + 4 * HPC, 2])

    P.emit("pool", lambda e: e.memset(M32[:], 0.0), w=[M32])
    P.emit("pool", lambda e: e.memset(Mb[:], 0.0), w=[Mb])
    P.emit("pool", lambda e: e.memset(car[:], 0.0), w=[car])
    P.emit("pool", lambda e: e.memset(ones[:], 1.0), w=[ones])

    wslot_i = [0]
    NPUMP = 5

    def load_w_cb(cb):
        i = wslot_i[0] % 6
        wslot_i[0] += 1
        dma(wsl[i][:, 0:KC * 128], winb[cb * 128:(cb + 1) * 128, :], [wsl[i]], ["winb"], f"d_w{i}")
        return wsl[i]

    def proj_mm(wt, bank, N):
        for kc in range(KC):
            mm(bank[:, 0:N], wt[:, kc * 128:(kc + 1) * 128], uT[:, kc, 0:N], [bank], [wt, uT],
               start=(kc == 0), stop=(kc == KC - 1))
            yield

    def proj(cb, bank, N):
        wt = load_w_cb(cb)
        for _ in proj_mm(wt, bank, N):
            pass

    def projA_gen(cb0, N):
        wts = [load_w_cb(cb0 + i) for i in range(4)]
        for i in range(4):
            yield from proj_mm(wts[i], PJ[i], N)

    def pump(gen, n):
        if gen is None:
            return
        for _ in range(n):
            try:
                next(gen)
            except StopIteration:
                return

    def shift_mix(bank, N, mucol, carry, out):
        cp("act", pb[:, 1:N + 1], bank[:, 0:N], [pb], [bank])
        cp("pool", pb[:, 0:1], carry, [pb], [car])
        ts("dve", tmp[:, 0:N], pb[:, 0:N], pv[:, mucol:mucol + 1], ALU.mult, [tmp], [pb, pv])
        stt(out[:, 0:N], pb[:, 1:N + 1], om[:, mucol:mucol + 1], tmp[:, 0:N], ALU.mult, ALU.add, [out], [pb, om, tmp])
        cp("pool", carry, pb[:, N:N + 1], [car], [pb])

    def phase1_tile(ti):
        nch = 1 if ti == 0 else 4
        N = nch * 128
        if ti == 0:
            dma(uT[:, :, 0:128], u_meta.ap().rearrange("p (k t) -> p k t", t=128), [uT], ["u_meta"], "d_ld")
        else:
            rb = (ti - 1) * 128
            dma(uT[:].rearrange("p k t -> p (k t)"), u_loc[rb:rb + 128, :], [uT], ["u_loc"], "d_ld")
        proj(0, PJ[0], N); proj(1, PJ[1], N)
        shift_mix(PJ[0], N, 0, car[:, 0, 0:1], zwm)
        act(tw[:, 0:N], zwm[:, 0:N], AF.Tanh, [tw], [zwm])
        shift_mix(PJ[1], N, 1, car[:, 1, 0:1], zwm)
        cp("act", zab[:, 0:N], zwm[:, 0:N], [zab], [zwm])
        for hp in range(HPC):
            pb0 = 2 + NPAR * hp
            cb0 = 2 + 8 * hp
            cr = 2 + 4 * hp
            hs = slice(hp * 128, (hp + 1) * 128)
            if hp == 0:
                for i in range(4):
                    proj(cb0 + i, PJ[i], N)
            shift_mix(PJ[0], N, pb0 + 0, car[:, cr + 0, 0:1], rz)
            shift_mix(PJ[1], N, pb0 + 1, car[:, cr + 1, 0:1], kz)
            shift_mix(PJ[2], N, pb0 + 2, car[:, cr + 2, 0:1], vz)
            act(sgA[:, 0:N], PJ[3][:, 0:N], AF.Sigmoid, [sgA], [PJ[3]])
            mm(PJ[0][:, 0:N], w2b[:, hs], tw[:, 0:N], [PJ[0]], [w2b, tw])
            act(Sx[:, 0:N], PJ[0][:, 0:N], AF.Sigmoid, [Sx], [PJ[0], pv], bias=pv[:, pb0 + 3:pb0 + 4])
            mm(PJ[1][:, 0:N], a2b[:, hs], zab[:, 0:N], [PJ[1]], [a2b, zab])
            act(aa[:, 0:N], PJ[1][:, 0:N], AF.Sigmoid, [aa], [PJ[1], pv], bias=pv[:, pb0 + 4:pb0 + 5])
            for ch in range(nch):
                cs = slice(ch * 128, (ch + 1) * 128)
                P.emit("dve", lambda e, cs=cs: e.tensor_tensor_scan(Sc[:, cs], ones[:, cs], Sx[:, cs], 0.0, ALU.mult, ALU.add),
                       w=[Sc], r=[ones, Sx])
            tt("pool", Sx[:, 0:N], Sc[:, 0:N], Sx[:, 0:N], ALU.subtract, [Sx], [Sc, Sx])
            act(eP[:, 0:N], Sc[:, 0:N], AF.Exp, [eP], [Sc], scale=-CDEC)
            act(ePx[:, 0:N], Sx[:, 0:N], AF.Exp, [ePx], [Sx], scale=-CDEC)
            act(eN[:, 0:N], Sc[:, 0:N], AF.Exp, [eN], [Sc], scale=CDEC)
            act(eEnd[:, 0:nch], Sc[:, 0:N].rearrange("p (c t) -> p c t", t=128)[:, :, 127], AF.Exp, [eEnd], [Sc], scale=-CDEC)
            ts("dve", kkr[:, 0:N], kz[:, 0:N], pv[:, pb0 + 5:pb0 + 6], ALU.mult, [kkr], [kz, pv])
            tt("pool", sqb[:, 0:N], kkr[:, 0:N], kkr[:, 0:N], ALU.mult, [sqb], [kkr])
            mm(PJ[2][:, 0:N], blk1b, sqb[:, 0:N], [PJ[2]], [cstb, sqb])
            act(t2[:, 0:N], PJ[2][:, 0:N], AF.Sqrt, [t2], [PJ[2]])
            ts("dve", t2[:, 0:N], t2[:, 0:N], 1e-12, ALU.max, [t2], [t2])
            P.emit("dve", lambda e: e.reciprocal(t3[:, 0:N], t2[:, 0:N]), w=[t3], r=[t2])
            tt("dve", kk[:, 0:N], kkr[:, 0:N], t3[:, 0:N], ALU.mult, [kk], [kkr, t3])
            ts("dve", t2[:, 0:N], aa[:, 0:N], pv[:, pb0 + 6:pb0 + 7], ALU.mult, [t2], [aa, pv, om],
               s2=om[:, pb0 + 6:pb0 + 7], op1=ALU.add)
            tt("pool", k2[:, 0:N], kz[:, 0:N], t2[:, 0:N], ALU.mult, [k2], [kz, t2])
            stt(bn[:, 0:N], aa[:, 0:N], -1.0, kk[:, 0:N], ALU.mult, ALU.mult, [bn], [aa, kk])
            v3 = lambda a: a[:, 0:N].rearrange("p (c t) -> p c t", t=128)
            tt("dve", KR[:, 0:nch, 0, :], v3(kk), v3(ePx), ALU.mult, [KR], [kk, ePx])
            tt("pool", KR[:, 0:nch, 1, :], v3(rz), v3(eP), ALU.mult, [KR], [rz, eP])
            tt("dve", kt[:, 0:N], k2[:, 0:N], eN[:, 0:N], ALU.mult, [kt], [k2, eN])
            tt("pool", bnt[:, 0:N], bn[:, 0:N], eN[:, 0:N], ALU.mult, [bnt], [bn, eN])
            cp("act", vb[:, 0:N], vz[:, 0:N], [vb], [vz])
            for ch in range(nch):
                cs = slice(ch * 128, (ch + 1) * 128)
                ts("dve", kh[:, cs], kt[:, cs], eEnd[:, ch:ch + 1], ALU.mult, [kh], [kt, eEnd])
                ts("pool", bh[:, cs], bnt[:, cs], eEnd[:, ch:ch + 1], ALU.mult, [bh], [bnt, eEnd])
            stt(t3[:, 0:N], rz[:, 0:N], pv[:, pb0 + 7:pb0 + 8], k2[:, 0:N], ALU.mult, ALU.mult, [t3], [rz, pv, k2])
            mm(PJ[3][:, 0:N], blk1, t3[:, 0:N], [PJ[3]], [cst, t3])
            tt("dve", bon[:, 0:N], vz[:, 0:N], PJ[3][:, 0:N], ALU.mult, [bon], [vz, PJ[3]])
            for i in range(4):
                proj(cb0 + 4 + i, PJ[i], N)
            cp("act", hb[:, 0:N], PJ[2][:, 0:N], [hb], [PJ[2]])
            cp("pool", ub[:, 0:2], car[:, cr + 3, 0:2], [ub], [car])
            tt("dve", ub[:, 2:N + 2], hb[:, 0:N], PJ[1][:, 0:N], ALU.mult, [ub], [hb, PJ[1]])
            ts("dve", acc[:, 0:N], ub[:, 0:N], pv[:, pb0 + 10:pb0 + 11], ALU.mult, [acc], [ub, pv])
            stt(acc[:, 0:N], ub[:, 1:N + 1], pv[:, pb0 + 11:pb0 + 12], acc[:, 0:N], ALU.mult, ALU.add, [acc], [ub, pv, acc])
            stt(acc[:, 0:N], ub[:, 2:N + 2], pv[:, pb0 + 12:pb0 + 13], acc[:, 0:N], ALU.mult, ALU.add, [acc], [ub, pv, acc])
            cp("pool", car[:, cr + 3, 0:2], ub[:, N:N + 2], [car], [ub])
            tt("dve", acc[:, 0:N], acc[:, 0:N], PJ[0][:, 0:N], ALU.mult, [acc], [acc, PJ[0]])
            act(hb[:, 0:N], PJ[3][:, 0:N], AF.Sigmoid, [hb], [PJ[3]])
            tt("pool", mB[:, 0:N], acc[:, 0:N], hb[:, 0:N], ALU.mult, [mB], [acc, hb])
            gen = projA_gen(cb0 + 8, N) if hp + 1 < HPC else None
            for e_ in range(2):
                mc = blk1[:, e_ * 64:e_ * 64 + 1]
                ts("pool", KRm[:, 0:nch, e_, :], KR[:, 0:nch, 0, :], mc, ALU.mult, [KRm], [KR, cst])
                ts("dve", bntm[:, e_, 0:N], bnt[:, 0:N], mc, ALU.mult, [bntm], [bnt, cst])
                ts("pool", ktm[:, e_, 0:N], kt[:, 0:N], mc, ALU.mult, [ktm], [kt, cst])
            for ch in range(nch):
                if SKIP_SCAN:
                    break
                cs = slice(ch * 128, (ch + 1) * 128)
                for i, src in enumerate([KR[:, ch, 0, :], vb[:, cs], kh[:, cs], bh[:, cs]]):
                    trp(pstr[:, i * 128:(i + 1) * 128], src, identb, [pstr], [src, cstb])
                cp("act", TM[:].rearrange("p a b -> p (a b)"), pstr[:, 0:512], [TM], [pstr])
                krc = KR[:, ch, :, :].rearrange("p a b -> p (a b)")
                for e_ in range(2):
                    mm(SC[:, e_ * 128:(e_ + 1) * 128], KRm[:, ch, e_, :], bnt[:, cs], [SC], [KRm, bnt])
                    mm(SA[:, e_ * 256:(e_ + 1) * 256], bntm[:, e_, cs], krc, [SA], [KR, bntm])
                    mm(SB_[:, e_ * 256:(e_ + 1) * 256], ktm[:, e_, cs], krc, [SB_], [KR, ktm])
                pump(gen, NPUMP)
                sa4 = SA[:, :].rearrange("p (e s t) -> p e s t", e=2, s=2)
                sc3 = SC[:, 0:256].rearrange("p (e t) -> p e t", e=2)
                for e_ in range(2):
                    tt("dve", N_[0][:, e_, :], sc3[:, e_, :], m_sl, ALU.mult, [N_[0]], [SC, cstb])
                    tt("dve", Q_[0][:, e_, 0, :], sa4[:, e_, 0, :], m_su, ALU.mult, [Q_[0]], [SA, cstb])
                    tt("dve", ArT[:, e_, :], sa4[:, e_, 1, :], m_ui, ALU.mult, [ArT], [SA, cstb])
                    tt("dve", BB[:, e_, :, :].rearrange("p a b -> p (a b)"), SB_[:, e_ * 256:(e_ + 1) * 256], m_suui, ALU.mult, [BB], [SB_, cstb])
                for e_ in range(2):
                    tt("pool", Q_[1][:, e_, 1, :], Q_[0][:, e_, 0, :], identb, ALU.add, [Q_[1]], [Q_[0], cstb])
                for e_ in range(2):
                    mm(SA[:, e_ * 256:e_ * 256 + 128], N_[0][:, e_, :], Q_[0][:, e_, 0, :], [SA], [N_[0], Q_[0]])
                    mm(SC[:, e_ * 128:(e_ + 1) * 128], Q_[0][:, e_, 0, :], N_[0][:, e_, :], [SC], [N_[0], Q_[0]])
                pump(gen, NPUMP)
                cp("act", Q_[1][:, :, 0, :], sa4[:, :, 0, :], [Q_[1]], [SA])
                cp("dve", N_[1][:], sc3, [N_[1]], [SC])
                cur = 1
                for lev in range(1, 7):
                    nxt = 1 - cur
                    last = lev == 6
                    for e_ in range(2):
                        mm(SA[:, e_ * 256:(e_ + 1) * 256], N_[cur][:, e_, :], Q_[cur][:, e_, :, :].rearrange("p a b -> p (a b)"),
                           [SA], [N_[cur], Q_[cur]])
                        if not last:
                            mm(SC[:, e_ * 128:(e_ + 1) * 128], Q_[cur][:, e_, 0, :], N_[cur][:, e_, :], [SC], [N_[cur], Q_[cur]])
                    pump(gen, NPUMP)
                    tt("dve", Q_[nxt][:, :, 1, :], sa4[:, :, 1, :], Q_[cur][:, :, 1, :], ALU.add, [Q_[nxt]], [SA, Q_[cur]])
                    if not last:
                        cp("act", Q_[nxt][:, :, 0, :], sa4[:, :, 0, :], [Q_[nxt]], [SA])
                        cp("act", N_[nxt][:], sc3, [N_[nxt]], [SC])
                    cur = nxt
                TTf = Q_[cur]
                cp("pool", XK[:, :, 0:64], TM[:, 0, :].rearrange("p (e j) -> p e j", e=2), [XK], [TM])
                for e_ in range(2):
                    mm(SC[:, 256 + e_ * 64:256 + (e_ + 1) * 64], BB[:, e_, 0, :], TM[:, 1, e_ * 64:(e_ + 1) * 64], [SC], [BB, TM])
                pump(gen, NPUMP)
                cp("act", XK[:, :, 64:128], SC[:, 256:384].rearrange("p (e j) -> p e j", e=2), [XK], [SC])
                for e_ in range(2):
                    mm(SB_[:, e_ * 128:(e_ + 1) * 128], TTf[:, e_, 1, :], XK[:, e_, :], [SB_], [TTf, XK])
                pump(gen, NPUMP)
                sb3 = SB_[:, 0:256].rearrange("p (e x) -> p e x", e=2)
                cp("act", WZ[:, 0, :].rearrange("p (e j) -> p e j", e=2), sb3[:, :, 0:64], [WZ], [SB_])
                cp("dve", WZ[:, 1, :].rearrange("p (e j) -> p e j", e=2), sb3[:, :, 64:128], [WZ], [SB_])
                Wall = WZ[:, 0, :]
                Zall = WZ[:, 1, :]
                mm(SA[:, 0:128], Wall, TM[:, 3, :], [SA], [WZ, TM])
                mm(SA[:, 128:256], TM[:, 3, :], Zall, [SA], [WZ, TM], start=True, stop=False)
                mm(SA[:, 128:256], TM[:, 2, :], TM[:, 1, :], [SA], [TM], start=False, stop=True)
                pump(gen, NPUMP)
                tt("dve", t4[:, 0:128], SA[:, 0:128], blk1, ALU.mult, [t4], [SA, cst])
                stt(GT[:], ident, eEnd[:, ch:ch + 1], t4[:, 0:128], ALU.mult, ALU.add, [GT], [cst, eEnd, t4])
                tt("dve", Hs[:], SA[:, 128:256], blk1, ALU.mult, [Hs], [SA, cst])
                for e_ in range(2):
                    mm(SB_[:, 256 + e_ * 128:256 + (e_ + 1) * 128], Wall, ArT[:, e_, :], [SB_], [WZ, ArT])
                for e_ in range(2):
                    pr = slice(e_ * 64, (e_ + 1) * 64)
                    tt("dve", RpT[pr, :], SB_[pr, 256 + e_ * 128:256 + (e_ + 1) * 128], KR[pr, ch, 1, :], ALU.add, [RpT], [SB_, KR])
                pump(gen, NPUMP)
                for e_ in range(2):
                    reg = SC[:, e_ * 128:(e_ + 1) * 128]
                    mm(reg, TM[:, 1, :], BB[:, e_, 1, :], [SC], [TM, BB], start=True, stop=False)
                    mm(reg, Zall, ArT[:, e_, :], [SC], [WZ, ArT], start=False, stop=False)
                    mm(reg, Mb[:, hp, :], RpT[:, :], [SC], [Mb, RpT], start=False, stop=True)
                for e_ in range(2):
                    pr = slice(e_ * 64, (e_ + 1) * 64)
                    cp("act", yy[pr, cs], SC[pr, e_ * 128:(e_ + 1) * 128], [yy], [SC])
                pump(gen, NPUMP)
                mm(SA[:, 256:384], GT[:], Mb[:, hp, :], [SA], [GT, Mb])
                pump(gen, NPUMP)
                tt("dve", M32[:, hp, :], SA[:, 256:384], Hs[:], ALU.add, [M32], [SA, Hs])
                cp("act", Mb[:, hp, :], M32[:, hp, :], [Mb], [M32])
            if SKIP_SCAN:
                P.emit("pool", lambda e: e.memset(yy[:], 0.0), w=[yy])
            pump(gen, 100000)
            mm(SA[:, 0:N], blka, yy[:, 0:N], [SA], [cst, yy])
            tt("dve", yc[:, 0:N], yy[:, 0:N], SA[:, 0:N], ALU.subtract, [yc], [yy, SA])
            tt("pool", t2[:, 0:N], yc[:, 0:N], yc[:, 0:N], ALU.mult, [t2], [yc])
            mm(SB_[:, 0:N], blka, t2[:, 0:N], [SB_], [cst, t2])
            act(t2[:, 0:N], SB_[:, 0:N], AF.Sqrt, [t2], [SB_, eps2], bias=eps2[:, 0:1])
            P.emit("dve", lambda e: e.reciprocal(kkr[:, 0:N], t2[:, 0:N]), w=[kkr], r=[t2])
            tt("dve", yc[:, 0:N], yc[:, 0:N], kkr[:, 0:N], ALU.mult, [yc], [yc, kkr])
            ts("dve", yc[:, 0:N], yc[:, 0:N], pv[:, pb0 + 8:pb0 + 9], ALU.mult, [yc], [yc, pv],
               s2=pv[:, pb0 + 9:pb0 + 10], op1=ALU.add)
            tt("pool", yc[:, 0:N], yc[:, 0:N], bon[:, 0:N], ALU.add, [yc], [yc, bon])
            tt("dve", mA[:, 0:N], yc[:, 0:N], sgA[:, 0:N], ALU.mult, [mA], [yc, sgA])
            if ti > 0:
                q, off = divmod((ti - 1) * 512, QT)
                r0 = (q * c.NT2 + off // 512) * 128
                dma(m_loc[r0:r0 + 128, (2 * hp) * 512:(2 * hp + 1) * 512], mA[:, 0:512], ["m_loc"], [mA], "d_st")
                dma(m_loc[r0:r0 + 128, (2 * hp + 1) * 512:(2 * hp + 2) * 512], mB[:, 0:512], ["m_loc"], [mB], "d_st")

    for ti in range(c.NT1 + 1):
        if stop_after == 1.0:
            break
        if stop_after == 1.2 and ti == 1:
            break
        if stop_after in (1.4, 1.5) and ti == 2:
            break
        phase1_tile(ti)
        if ti == 1 and stop_after != 1.4:
            import os
            _m = os.environ.get("DBG_AG", "ouw")
            if "o" in _m: gather_chunked(wo_loc, wo_all, c.NBO * 128, c.KOl * 512)
            if "u" in _m: gather_chunked(wu_loc, wu_all, c.HPR * c.NSP * c.NKH * 128, c.PKu * c.SPW)
            if "w" in _m: gather_chunked(wd_loc, wd_all, c.HPR * c.NBO * 128, c.FRC * 512)
    if stop_after not in (1.0, 1.2, 1.3, 1.4, 1.5):
        gather_chunked(m_loc, m_all, 4 * c.NT2 * 128, 2 * HPC * 512)
    P.finish_phase()
    P.replay(nc, sems)
    s1.close()
    if 1 <= stop_after < 2:
        es.close()
        return nc

    s2 = ExitStack()
    h = sb(s2, "h", [128, 4, D])
    RB = sb(s2, "RB", [128, c.KO * 512], BF16)
    mT = RB[:, :].rearrange("p (k t) -> p k t", t=512)
    uT2 = RB[:, 0:KC * 512].rearrange("p (k t) -> p k t", t=512)
    hidT = RB[:, KC * 512:(KC + c.FRC) * 512].rearrange("p (k t) -> p k t", t=512)
    junk2 = RB[:, (KC + c.FRC) * 512:(KC + c.FRC) * 512 + D]
    assert (KC + c.FRC) * 512 + D <= c.KO * 512
    ALIAS = ["mT", "uT2", "hidT", "junk2"]
    fsc = sb(s2, "fsc", [128, 1])
    def fence():
        P.emit("dve", lambda e: e.memset(fsc[:], 0.0), w=ALIAS + [fsc])
    rtmp = sb(s2, "rtmp", [128, 512])
    gfb = sb(s2, "gfb", [128, 1024])
    ss2 = sb(s2, "ss2", [128, 1]); rt2 = sb(s2, "rt2", [128, 1]); rstd2 = sb(s2, "rstd2", [128, 4])
    diag2 = sb(s2, "diag2", [128, 128])
    wp = [sb(s2, f"wp{i}", [128, c.WS], BF16) for i in range(2)]
    wpi = [0]
    qcache = {}

    def wslot():
        i = wpi[0] % 2
        wpi[0] += 1
        return wp[i], f"d_w{i}"

    def phase2_tile(tt_):
        t0 = tt_ * 512
        if stop_after in (2.0, 2.01):
            return
        for s in range(4):
            dma(h[:, s, :], x_d[t0 + s * 128:t0 + (s + 1) * 128, :], [h], [], "d_x")
        if stop_after == 2.05:
            return

        ld_eng = "sp" if tt_ < 2 else "act"

        Rm, Cm = 4 * c.NT2 * 128, 2 * HPC * 512
        rcm = rcof(Rm, Cm)
        nseg = max(1, 128 // rcm)
        assert rcm >= 128 and rcm % 128 == 0 or 128 % rcm == 0

        def ld_m(e):
            if ld_eng not in qcache:
                qcache[ld_eng] = (e.partition_id() % 4)
            qq = qcache[ld_eng]
            last = None
            if rcm >= 128:
                bpc = rcm // 128
                assert c.NT2 % bpc == 0 or bpc % c.NT2 == 0
                for g in range(4):
                    if bpc >= 4 * c.NT2:
                        base = g * rcm + tt_ * 128
                        dyn = qq * (c.NT2 * 128)
                        span = 3 * c.NT2 * 128
                    else:
                        assert c.NT2 % bpc == 0
                        base = (tt_ // bpc) * 4 * rcm + g * rcm + (tt_ % bpc) * 128
                        dyn = qq * ((c.NT2 // bpc) * 4 * rcm)
                        span = 3 * (c.NT2 // bpc) * 4 * rcm
                    win = m_all.ap()[base:base + span + 128, :]
                    if last is not None:
                        last.then_inc(sems["d_ld"], 16)
                    last = e.dma_start(out=RB[:, g * Cm:(g + 1) * Cm], in_=win[bass.ds(dyn, 128), :])
                return last, 4
            mv = m_all.ap().rearrange("(k g r) c -> k g r c", g=4, r=rcm)
            for h_ in range(nseg):
                k0 = tt_ * nseg + h_
                win = mv[k0:k0 + 3 * c.NT2 * nseg + 1]
                src = win[bass.ds(qq * (c.NT2 * nseg), 1)].rearrange("k g r c -> (k r) g c")
                if last is not None:
                    last.then_inc(sems["d_ld"], 16)
                last = e.dma_start(out=RB[h_ * rcm:(h_ + 1) * rcm, :].rearrange("p (g c) -> p g c", g=4), in_=src)
            return last, nseg

        ndm = 4 if rcm >= 128 else nseg
        P.emit(ld_eng, lambda e: ld_m(e)[0], w=["mT"], r=["m_all"], sig="d_ld", inc=16)
        P.cnt["d_ld"] += 16 * (ndm - 1)
        if stop_after == 2.1:
            return
        for n in range(c.NBO):
            for kq in range(c.NKQ):
                wt, sg = wslot()
                for rr in range(c.RPP):
                    rk = kq * c.RPP + rr
                    gload(wt, rr * c.KOl * 512, c.KOl * 512, wo_all, rk, n * 128, c.NBO * 128, c.KOl * 512, sg)
                for s in range(4):
                    for k in range(c.PK):
                        kcg = kq * c.PK + k
                        mm(PJ[s][:, :], mT[:, kcg, s * 128:(s + 1) * 128], wt[:, k * 512:(k + 1) * 512], [PJ[s]], ["mT", wt],
                           start=(kq == 0 and k == 0), stop=(kq == c.NKQ - 1 and k == c.PK - 1))
            for s in range(4):
                tt("dve", h[:, s, n * 512:(n + 1) * 512], h[:, s, n * 512:(n + 1) * 512], PJ[s][:, :], ALU.add, [h], [h, PJ[s]])
        if stop_after == 2.2:
            return
        fence()
        for s in range(4):
            rms_to_uT(h[:, s, :], GMLP, uT2, s * 128, junk2, ss2[:], rt2[:], rstd2[:, s:s + 1], diag2[:], uTk="uT2", junkk="junk2")
        nhc = c.SPW // 128
        for j in range(8):
            for sp_ in range(c.NSP):
                for kh_ in range(c.NKH):
                    wt, sg = wslot()
                    gload(wt, 0, c.PKu * c.SPW, wu_all, j // c.HPR, (((j % c.HPR) * c.NSP + sp_) * c.NKH + kh_) * 128,
                          c.HPR * c.NSP * c.NKH * 128, c.PKu * c.SPW, sg)
                    for hc in range(nhc):
                        for k in range(c.PKu):
                            kc = kh_ * c.PKu + k
                            mm(PJ[hc][:, :], wt[:, k * c.SPW + hc * 128:k * c.SPW + (hc + 1) * 128], uT2[:, kc, :], [PJ[hc]], [wt, "uT2"],
                               start=(kh_ == 0 and k == 0), stop=(kh_ == c.NKH - 1 and k == c.PKu - 1))
                for hc in range(nhc):
                    act(rtmp[:], PJ[hc][:, :], AF.Relu, [rtmp], [PJ[hc]])
                    tt("pool", hidT[:, sp_ * nhc + hc, :], rtmp[:], rtmp[:], ALU.mult, ["hidT"], [rtmp])
            for n in range(c.NBO):
                wt, sg = wslot()
                gload(wt, 0, c.FRC * 512, wd_all, j // c.HPR, ((j % c.HPR) * c.NBO + n) * 128, c.HPR * c.NBO * 128, c.FRC * 512, sg)
                for s in range(4):
                    for k in range(c.FRC):
                        mm(PJ[s][:, :], hidT[:, k, s * 128:(s + 1) * 128], wt[:, k * 512:(k + 1) * 512], [PJ[s]], ["hidT", wt],
                           start=(k == 0), stop=(k == c.FRC - 1))
                for s in range(4):
                    tt("dve", h[:, s, n * 512:(n + 1) * 512], h[:, s, n * 512:(n + 1) * 512], PJ[s][:, :], ALU.add, [h], [h, PJ[s]])
        if stop_after == 2.3:
            return
        for s in range(4):
            act(junk2[:], h[:, s, :], AF.Square, ["junk2", ss2], [h], accum=ss2[:])
            act(rt2[:], ss2[:], AF.Sqrt, [rt2], [ss2, eps1], bias=eps1[:, 0:1], scale=1.0 / D)
            P.emit("dve", lambda e, s=s: e.reciprocal(rstd2[:, s:s + 1], rt2[:]), w=[rstd2], r=[rt2])
        for cq in range(D // 1024 if D >= 1024 else 1):
            cw = min(1024, D)
            dma(gfb[:, 0:cw], gf_d[:, cq * cw:(cq + 1) * cw].partition_broadcast(128), [gfb], [], "d_ld")
            for s in range(4):
                stt(h[:, s, cq * cw:(cq + 1) * cw], h[:, s, cq * cw:(cq + 1) * cw], rstd2[:, s:s + 1], gfb[:, 0:cw],
                    ALU.mult, ALU.mult, [h], [h, rstd2, gfb])
        fence()
        for s in range(4):
            dma(out_d[t0 + s * 128:t0 + (s + 1) * 128, :], h[:, s, :], ["out"], [h], "d_out")

    for tt_ in range(c.NT2):
        phase2_tile(tt_)
    P.finish_phase()
    if stop_after != 2.01:
        P.replay(nc, sems)
    s2.close()
    es.close()
    return nc


def make_consts():
    cst = np.zeros((128, 1024), np.float32)
    cst[:, 0:128] = np.eye(128)
    p = np.arange(128)
    blk = (p[:, None] // 64 == p[None, :] // 64).astype(np.float32)
    cst[:, 128:256] = blk
    cst[:, 256:384] = blk / 64.0
    cst[:, 384:448] = (p[:, None] % 64 == np.arange(64)[None, :]).astype(np.float32)
    cst[:, 512:640] = (p[None, :] < p[:, None])
    cst[:, 640:768] = (p[None, :] > p[:, None])
    cst[:, 768:896] = (p[None, :] >= p[:, None])
    return cst


def prep_inputs(cfg, inp):
    c = cfg
    D, KC, HPC = c.D, c.KC, c.HPC
    f = lambda a: np.ascontiguousarray(np.asarray(a, dtype=np.float32))
    x = f(inp["x"]); w_in = f(inp["w_in"])[0]; w_out = f(inp["w_out"])[0]
    w_up = f(inp["w_up"])[0]; w_down = f(inp["w_down"])[0]
    mu = f(inp["rwkv_shift_mu"])[0]
    w0 = f(inp["rwkv_w0"])[0]; a0 = f(inp["rwkv_a0"])[0]; k_k = f(inp["rwkv_k_k"])[0]; k_a = f(inp["rwkv_k_a"])[0]
    r_k = f(inp["rwkv_r_k"])[0].reshape(-1); ln_w = f(inp["rwkv_ln_w"])[0]; ln_b = f(inp["rwkv_ln_b"])[0]
    cw = f(inp["conv_w"])[0]
    w2 = f(inp["rwkv_w2"])[0]; a2 = f(inp["rwkv_a2"])[0]
    g_mix = f(inp["norm_mix_g"])[0]; g_mlp = f(inp["norm_mlp_g"])[0]; gf = f(inp["norm_final_g"]).reshape(1, D)
    meta = f(inp["meta_tokens"])
    NR = c.NR
    cst = make_consts()
    perm = []
    for g in range(4):
        for hp in range(HPC):
            gp = g * HPC + hp
            perm.append(np.arange(gp * 128, (gp + 1) * 128))
            perm.append(D + np.arange(gp * 128, (gp + 1) * 128))
    perm = np.concatenate(perm)
    w_out_p = w_out[perm]
    maps = []
    for core in range(8):
        b, g = divmod(core, 4)
        q = g
        cols = [np.arange(3 * D, 3 * D + 128), np.arange(3 * D + 128, 3 * D + 256)]
        pvc = [mu[3 * D:3 * D + 128], mu[3 * D + 128:3 * D + 256]]
        for hp in range(HPC):
            gp = g * HPC + hp
            ch = np.arange(gp * 128, (gp + 1) * 128)
            cols += [ch, D + ch, 2 * D + ch, NR + 3 * D + ch, NR + ch, NR + D + ch, NR + 2 * D + ch, NR + 4 * D + ch]
            pvc += [mu[ch], mu[D + ch], mu[2 * D + ch], w0[ch], a0[ch], k_k[ch], k_a[ch], r_k[ch], ln_w[ch], ln_b[ch],
                    cw[0][ch], cw[1][ch], cw[2][ch]]
        pvm = np.stack(pvc, axis=1)
        pvm = np.concatenate([pvm, g_mix.reshape(KC, 128).T, g_mlp.reshape(KC, 128).T], axis=1)
        cols = np.concatenate(cols)
        ws = w_in[:, cols]
        win = ws.reshape(KC, 128, c.NCB, 128).transpose(2, 1, 0, 3).reshape(c.NCB * 128, KC * 128)
        chs = np.arange(g * HPC * 128, (g + 1) * HPC * 128)
        r = g
        wo_r = w_out_p[r * c.KOl * 128:(r + 1) * c.KOl * 128]
        wo = wo_r.reshape(c.KOl, 128, c.NBO, 512).transpose(2, 1, 0, 3).reshape(c.NBO * 128, c.KOl * 512)
        wu_r = w_up[:, r * c.HPR * c.FR:(r + 1) * c.HPR * c.FR]
        wu = wu_r.reshape(c.NKH, c.PKu, 128, c.HPR, c.NSP, c.SPW).transpose(3, 4, 0, 2, 1, 5).reshape(c.HPR * c.NSP * c.NKH * 128, c.PKu * c.SPW)
        wd_r = w_down[r * c.HPR * c.FR:(r + 1) * c.HPR * c.FR]
        wd = wd_r.reshape(c.HPR, c.FRC, 128, c.NBO, 512).transpose(0, 3, 2, 1, 4).reshape(c.HPR * c.NBO * 128, c.FRC * 512)
        maps.append({
            "x": np.ascontiguousarray(x[b, q * c.QT:(q + 1) * c.QT]), "xb": np.ascontiguousarray(x[b]),
            "meta": meta, "win": np.ascontiguousarray(win), "wo": np.ascontiguousarray(wo),
            "wu": np.ascontiguousarray(wu), "wd": np.ascontiguousarray(wd),
            "pv": np.ascontiguousarray(pvm), "w2": np.ascontiguousarray(w2[:, chs]), "a2": np.ascontiguousarray(a2[:, chs]),
            "gf": gf, "cst": cst,
        })
    return maps


def run_cfg(cfg, inp, stop_after=9, skip_scan=False):
    nc = build(cfg, stop_after, skip_scan)
    maps = prep_inputs(cfg, inp)
    res = run_bass_kernel_spmd(nc, maps, core_ids=list(range(8)))
    out = np.zeros((2, cfg.SEQ, cfg.D), np.float32)
    for core in range(8):
        b, q = divmod(core, 4)
        out[b, q * cfg.QT:(q + 1) * cfg.QT] = res.results[core]["out"]
    return out


def kernel(**inputs):
    return run_cfg(FULL, inputs)
```

```python
import numpy as np
import ml_dtypes
import concourse.bass as bass
import concourse.mybir as mybir
from concourse.bass_utils import run_bass_kernel_spmd

F32 = mybir.dt.float32
BF16 = mybir.dt.bfloat16
AF = mybir.ActivationFunctionType
ALU = mybir.AluOpType
NPAR = 13
CDEC = 0.6065306597126334


class Cfg:
    def __init__(s, D, SEQ, PK):
        s.D = D; s.SEQ = SEQ; s.KC = D // 128; s.NH = D // 64; s.NP = s.NH // 2; s.HPC = s.NP // 4
        s.DFF = 4 * D; s.FR = s.DFF // 8; s.FRC = s.FR // 128
        s.QT = SEQ // 4; s.NT2 = s.QT // 512; s.NT1 = SEQ // 512
        s.NCB = 2 + 8 * s.HPC; s.MC = 2 * s.HPC * 128
        s.KO = 2 * s.KC; s.KOl = s.KO // 4; s.HPR = 2; s.CH = 524288; s.PK = PK; s.RPP = PK // s.KOl; s.NKQ = s.KO // PK
        s.PKu = min(PK, s.KC); s.NKH = s.KC // s.PKu
        s.SPW = min(512, s.FR); s.NSP = s.FR // s.SPW
        s.NBO = D // 512
        s.NR = 3 * D + 256
        s.NPV = 2 + NPAR * s.HPC + 2 * s.KC
        s.WS = max(PK * 512, s.PKu * s.SPW, s.FRC * 512, s.KC * 128)


FULL = Cfg(4096, 8192, 16)


class Prog:
    ENGS = ["pe", "act", "dve", "pool", "sp"]

    def __init__(self):
        self.streams = {e: [] for e in self.ENGS}
        self.cnt = {}
        self.lastw = {}
        self.readers = {}
        self.waited = {e: {} for e in self.ENGS}

    @staticmethod
    def key(k):
        if isinstance(k, str):
            return k
        if hasattr(k, "tensor"):
            return k.tensor.name
        return k.name

    def emit(self, eng, fn, w=(), r=(), sig=None, inc=1):
        sig = sig or ("e_" + eng)
        deps = {}

        def add(d):
            if d is not None:
                s, v = d
                if s.startswith("d_") or s.startswith("cc"):
                    v = self.cnt[s]
                deps[s] = max(deps.get(s, 0), v)

        rk = [self.key(k) for k in r]
        wk = [self.key(k) for k in w]
        for k in rk:
            add(self.lastw.get(k))
        for k in wk:
            add(self.lastw.get(k))
            for s, v in self.readers.get(k, {}).items():
                add((s, v))
        waits = []
        for s, v in deps.items():
            if s == "e_pe" and eng == "pe":
                continue
            if self.waited[eng].get(s, 0) < v:
                waits.append((s, v))
                self.waited[eng][s] = v
        self.cnt[sig] = self.cnt.get(sig, 0) + inc
        val = self.cnt[sig]
        self.streams[eng].append((waits, fn, sig, inc))
        for k in rk:
            d = self.readers.setdefault(k, {})
            d[sig] = max(d.get(sig, 0), val)
        for k in wk:
            self.lastw[k] = (sig, val)
            self.readers[k] = {}
        return val

    def wait_all(self, eng, sig):
        v = self.cnt.get(sig, 0)
        if self.waited[eng].get(sig, 0) < v:
            self.waited[eng][sig] = v
            self.streams[eng].append(([(sig, v)], None, None, 0))

    def finish_phase(self):
        waits = [(s, v) for s, v in self.cnt.items() if self.waited["sp"].get(s, 0) < v and s != "e_sp"]
        for s, v in waits:
            self.waited["sp"][s] = v
        self.streams["sp"].append((waits, None, None, 0))

    def replay(self, nc, sems):
        streams = self.streams
        self.streams = {e: [] for e in self.ENGS}

        def run(e, items):
            for waits, fn, sig, inc in items:
                for s, v in waits:
                    e.wait_ge(sems[s], v)
                if fn is not None:
                    ins = fn(e)
                    ins.then_inc(sems[sig], inc)

        with nc.Block() as block:
            @block.tensor
            def _(e):
                run(e, streams["pe"])

            @block.scalar
            def _(e):
                run(e, streams["act"])

            @block.vector
            def _(e):
                run(e, streams["dve"])

            @block.gpsimd
            def _(e):
                run(e, streams["pool"])

            @block.sync
            def _(e):
                run(e, streams["sp"])


SEM_NAMES = ["e_pe", "e_act", "e_dve", "e_pool", "e_sp", "d_ld", "d_st", "d_w0", "d_w1", "d_w2", "d_w3", "d_w4", "d_w5", "d_x", "d_cast",
             "cc", "d_out", "d_u", "d_x2"]


def build(cfg, stop_after=9, SKIP_SCAN=False):
    c = cfg
    D, KC, HPC, QT, SEQ = c.D, c.KC, c.HPC, c.QT, c.SEQ
    nc = bass.Bass("TRN2", target_bir_lowering=False)
    P = Prog()

    def din(name, shape, dt=F32):
        return nc.dram_tensor(name, list(shape), dt, kind="ExternalInput").ap()

    x_d = din("x", [QT, D]); xb_d = din("xb", [SEQ, D]); meta_d = din("meta", [16, D])
    win_d = din("win", [c.NCB * 128, KC * 128])
    wo_d = din("wo", [c.NBO * 128, c.KOl * 512])
    wu_d = din("wu", [c.HPR * c.NSP * c.NKH * 128, c.PKu * c.SPW])
    wd_d = din("wd", [c.HPR * c.NBO * 128, c.FRC * 512])
    pv_d = din("pv", [128, c.NPV]); w2_d = din("w2", [128, HPC * 128]); a2_d = din("a2", [128, HPC * 128])
    gf_d = din("gf", [1, D]); cst_d = din("cst", [128, 1024])
    out_d = nc.dram_tensor("out", [QT, D], F32, kind="ExternalOutput").ap()

    def dint(name, shape, dt=BF16):
        return nc.dram_tensor(name, list(shape), dt)

    winb = dint("winb", [c.NCB * 128, KC * 128])
    u_loc = dint("u_loc", [(SEQ // 512) * 128, KC * 512])
    u_meta = dint("u_meta", [128, KC * 128])
    wo_loc = dint("wo_loc", [c.NBO * 128, c.KOl * 512]); wo_all = dint("wo_all", [4 * c.NBO * 128, c.KOl * 512])
    wu_loc = dint("wu_loc", [c.HPR * c.NSP * c.NKH * 128, c.PKu * c.SPW])
    wu_all = dint("wu_all", [8 * c.NSP * c.NKH * 128, c.PKu * c.SPW])
    wd_loc = dint("wd_loc", [c.HPR * c.NBO * 128, c.FRC * 512]); wd_all = dint("wd_all", [8 * c.NBO * 128, c.FRC * 512])
    m_loc = dint("m_loc", [4 * c.NT2 * 128, 2 * HPC * 512]); m_all = dint("m_all", [16 * c.NT2 * 128, 2 * HPC * 512])

    from contextlib import ExitStack
    es = ExitStack()
    sems = {n: es.enter_context(nc.semaphore(n)) for n in SEM_NAMES}

    def sb(stack, name, shape, dt=F32):
        return stack.enter_context(nc.sbuf_tensor("s_" + name, list(shape), dt))

    def ps(stack, name, shape, dt=F32):
        return stack.enter_context(nc.psum_tensor(name, list(shape), dt))

    def mm(out, lhsT, rhs, w, r, start=True, stop=True):
        P.emit("pe", lambda e: e.matmul(out, lhsT=lhsT, rhs=rhs, start=start, stop=stop), w=w, r=r)

    def trp(out, in_, ident, w, r):
        P.emit("pe", lambda e: e.transpose(out, in_, ident), w=w, r=r)

    def act(out, in_, func, w, r, bias=None, scale=None, accum=None):
        kw = {}
        if bias is not None: kw["bias"] = bias
        if scale is not None: kw["scale"] = scale
        if accum is not None: kw["accum_out"] = accum
        P.emit("act", lambda e: e.activation(out, in_, func, **kw), w=w, r=r)

    def tt(eng, out, in0, in1, op, w, r):
        P.emit(eng, lambda e: e.tensor_tensor(out, in0, in1, op), w=w, r=r)

    def ts(eng, out, in0, s1, op0, w, r, s2=None, op1=None):
        if op1 is None:
            P.emit(eng, lambda e: e.tensor_scalar(out, in0, s1, None, op0), w=w, r=r)
        else:
            P.emit(eng, lambda e: e.tensor_scalar(out, in0, s1, s2, op0, op1), w=w, r=r)

    def stt(out, in0, scalar, in1, op0, op1, w, r):
        P.emit("dve", lambda e: e.scalar_tensor_tensor(out, in0, scalar, in1, op0, op1), w=w, r=r)

    def cp(eng, out, in_, w, r):
        if eng == "act":
            P.emit("act", lambda e: e.copy(out, in_), w=w, r=r)
        else:
            P.emit(eng, lambda e: e.tensor_copy(out, in_), w=w, r=r)

    def dma(out, in_, w, r, sig, eng="sp"):
        P.emit(eng, lambda e: e.dma_start(out=out, in_=in_), w=w, r=r, sig=sig, inc=16)

    def allgather(in_t, out_t, groups):
        P.emit("pool", lambda e: e.collective_compute("AllGather", ALU.bypass, replica_groups=groups,
                                                      ins=[in_t.ap().opt()], outs=[out_t.ap().opt()]),
               w=[out_t.name], r=[in_t.name], sig="cc", inc=1)

    G4 = [[0, 1, 2, 3], [4, 5, 6, 7]]

    def rcof(R, C):
        return min(R, max(1, c.CH // C))

    def gather_chunked(loc_t, all_t, R, C, k0=None, k1=None):
        rc = rcof(R, C)
        for k in range(0 if k0 is None else k0, R // rc if k1 is None else k1):
            P.emit("pool", lambda e, k=k: e.collective_compute(
                "AllGather", ALU.bypass, replica_groups=G4,
                ins=[loc_t[k * rc:(k + 1) * rc, :].opt()], outs=[all_t[k * 4 * rc:(k + 1) * 4 * rc, :].opt()]),
                w=[], r=[loc_t.name], sig="cc", inc=1)

    def gsegs(r, i0, n, R, C):
        rc = rcof(R, C)
        out = []
        i = i0
        while i < i0 + n:
            k, o = divmod(i, rc)
            nr = min(rc - o, i0 + n - i)
            out.append((i - i0, k * 4 * rc + r * rc + o, nr))
            i += nr
        return out

    def gload(dst_tile, col0, ncols, all_t, r, i0, R, C, sg):
        for po, src, nr in gsegs(r, i0, 128, R, C):
            dma(dst_tile[po:po + nr, col0:col0 + ncols], all_t[src:src + nr, :], [dst_tile], [all_t.name], sg)

    G8 = [list(range(8))]

    pv = sb(es, "pv", [128, c.NPV]); om = sb(es, "om", [128, c.NPV])
    cst = sb(es, "cst", [128, 1024])
    cstb = sb(es, "cstb", [128, 1024], BF16)
    eps1 = sb(es, "eps1", [128, 1]); eps2 = sb(es, "eps2", [128, 1])
    ident = cst[:, 0:128]
    blk1 = cst[:, 128:256]
    blka = cst[:, 256:384]
    idblk = cst[:, 384:448]
    identb = cstb[:, 0:128]
    blk1b = cstb[:, 128:256]
    m_sl = cstb[:, 512:640]
    m_su = cstb[:, 640:768]
    m_ui = cstb[:, 768:896]
    m_suui = cstb[:, 640:896]

    psb = [ps(es, f"psb{i}", [128, 512]) for i in range(7)]
    pstr = ps(es, "pstr", [128, 1024], BF16)
    PJ = psb[0:4]; SA = psb[4]; SB_ = psb[5]; SC = psb[6]

    dma(pv[:], pv_d, [pv], [], "d_ld")
    dma(cst[:], cst_d, [cst], [], "d_ld")
    cp("dve", cstb[:], cst[:], [cstb], [cst])
    ts("dve", om[:], pv[:], -1.0, ALU.mult, [om], [pv], s2=1.0, op1=ALU.add)
    P.emit("dve", lambda e: e.memset(eps1[:], 1e-6), w=[eps1])
    P.emit("dve", lambda e: e.memset(eps2[:], 64e-5), w=[eps2])

    GMIX = 2 + NPAR * HPC
    GMLP = GMIX + KC

    def rms_to_uT(xs, gcol0, uT, col0, junk, ss, rt, rstd, diag, uTk=None, junkk=None):
        uTk = uTk or uT
        junkk = junkk or junk
        act(junk, xs, AF.Square, [junkk, ss], [xs], accum=ss)
        act(rt, ss, AF.Sqrt, [rt], [ss, eps1], bias=eps1[:, 0:1], scale=1.0 / D)
        P.emit("dve", lambda e: e.reciprocal(rstd, rt), w=[rstd], r=[rt])
        ts("dve", diag, ident, rstd[:, 0:1], ALU.mult, [diag], [cst, rstd])
        for kc in range(KC):
            bank = PJ[kc % 4]
            reg = bank[:, 0:128]
            key = bank
            mm(reg, xs[:, kc * 128:(kc + 1) * 128], diag, [key], [xs, diag])
            act(uT[:, kc, col0:col0 + 128], reg, AF.Copy, [uTk], [key, pv],
                scale=pv[:, gcol0 + kc:gcol0 + kc + 1])
        return rstd

    s1 = ExitStack()
    NT = 512
    uT = sb(s1, "uT", [128, KC, NT], BF16)
    xs = sb(s1, "xs", [128, D])
    xs2 = sb(s1, "xs2", [128, D])
    junk = sb(s1, "junk", [128, D], BF16)
    ss = sb(s1, "ss", [128, 1]); rt = sb(s1, "rt", [128, 1]); rstd = sb(s1, "rstd", [128, 1])
    diag = sb(s1, "diag", [128, 128])

    ncast = [0]

    def cast2d(dst_t, src_ap, rows, cols):
        for r0 in range(0, rows, 128):
            for c0 in range(0, cols, 2048):
                cw = min(2048, cols - c0)
                dma(dst_t[r0:r0 + 128, c0:c0 + cw], src_ap[r0:r0 + 128, c0:c0 + cw], [dst_t.name], [], "d_cast", eng="pool")
                ncast[0] += 1
                if ncast[0] % 16 == 0:
                    P.wait_all("pool", "d_cast")

    cast2d(winb, win_d, c.NCB * 128, KC * 128)
    cast2d(wo_loc, wo_d, c.NBO * 128, c.KOl * 512)
    cast2d(wu_loc, wu_d, c.HPR * c.NSP * c.NKH * 128, c.PKu * c.SPW)
    cast2d(wd_loc, wd_d, c.HPR * c.NBO * 128, c.FRC * 512)

    for t4 in range(SEQ // 512):
        for s in range(4):
            r0 = t4 * 512 + s * 128
            xt_ = xs if s % 2 == 0 else xs2
            dma(xt_[:], xb_d[r0:r0 + 128, :], [xt_], [], "d_x" if s % 2 == 0 else "d_x2")
            rms_to_uT(xt_[:], GMIX, uT, s * 128, junk[:], ss[:], rt[:], rstd[:], diag[:])
        dma(u_loc[t4 * 128:(t4 + 1) * 128, :], uT[:].rearrange("p k t -> p (k t)"), ["u_loc"], [uT], "d_u")
    P.emit("pool", lambda e: e.memset(xs[:], 0.0), w=[xs])
    dma(xs[112:128, :], meta_d, [xs], [], "d_x")
    rms_to_uT(xs[:], GMIX, uT, 0, junk[:], ss[:], rt[:], rstd[:], diag[:])
    dma(u_meta.ap().rearrange("p (k t) -> p k t", t=128), uT[:, :, 0:128], ["u_meta"], [uT], "d_u")
    P.finish_phase()
    P.replay(nc, sems)
    s1.close()
    if stop_after == 0:
        es.close()
        return nc

    s1 = ExitStack()
    uT = sb(s1, "uT1", [128, KC, NT], BF16)
    wsl = [sb(s1, f"wsl{i}", [128, KC * 128], BF16) for i in range(6)]
    w2b = sb(s1, "w2b", [128, HPC * 128], BF16); a2b = sb(s1, "a2b", [128, HPC * 128], BF16)
    dma(w2b[:], w2_d, [w2b], [], "d_cast", eng="pool")
    dma(a2b[:], a2_d, [a2b], [], "d_cast", eng="pool")

    def f32t(name, n=NT + 2):
        return sb(s1, name, [128, n])

    def b16t(name, n=NT):
        return sb(s1, name, [128, n], BF16)

    pb = f32t("pb"); tmp = f32t("tmp"); rz = f32t("rz"); kz = f32t("kz"); vz = f32t("vz")
    zwm = f32t("zwm"); sgA = f32t("sgA"); Sc = f32t("Sc"); Sx = f32t("Sx"); aa = f32t("aa")
    kkr = f32t("kkr"); kk = f32t("kk"); k2 = f32t("k2"); bn = f32t("bn"); t2 = f32t("t2")
    eP = f32t("eP"); ePx = f32t("ePx"); eN = f32t("eN"); yy = f32t("yy"); yc = f32t("yc"); t3 = f32t("t3")
    ub = f32t("ub"); hb = f32t("hb"); acc = f32t("acc")
    eEnd = sb(s1, "eEnd", [128, 4])
    ones = f32t("ones")
    tw = b16t("tw"); zab = b16t("zab"); sqb = b16t("sqb"); vb = b16t("vb")
    KR = sb(s1, "KR", [128, 4, 2, 128], BF16)
    kt = b16t("kt"); bnt = b16t("bnt"); kh = b16t("kh"); bh = b16t("bh")
    mA = b16t("mA"); mB = b16t("mB")
    N_ = [sb(s1, f"Nn{i}", [128, 2, 128], BF16) for i in range(2)]
    Q_ = [sb(s1, f"Qq{i}", [128, 2, 2, 128], BF16) for i in range(2)]
    ArT = sb(s1, "ArT", [128, 2, 128], BF16)
    BB = sb(s1, "BB", [128, 2, 2, 128], BF16)
    TM = sb(s1, "TM", [128, 4, 128], BF16)
    XK = sb(s1, "XK", [128, 2, 128], BF16)
    WZ = sb(s1, "WZ", [128, 2, 128], BF16)
    GT = sb(s1, "GT", [128, 128], BF16)
    Hs = sb(s1, "Hs", [128, 128])
    t4 = f32t("t4")
    bon = f32t("bon")
    KRm = sb(s1, "KRm", [128, 4, 2, 128], BF16)
    bntm = sb(s1, "bntm", [128, 2, NT], BF16); ktm = sb(s1, "ktm", [128, 2, NT], BF16)
    RpT = sb(s1, "RpT", [128, 128], BF16)
    M32 = sb(s1, "M32", [128, HPC, 128]); Mb = sb(s1, "Mb", [128, HPC, 128], BF16)
    car = sb(s1, "car", [128, 2 + 4 * HPC, 2])

    P.emit("pool", lambda e: e.memset(M32[:], 0.0), w=[M32])
    P.emit("pool", lambda e: e.memset(Mb[:], 0.0), w=[Mb])
    P.emit("pool", lambda e: e.memset(car[:], 0.0), w=[car])
    P.emit("pool", lambda e: e.memset(ones[:], 1.0), w=[ones])

    wslot_i = [0]
    NPUMP = 5

    def load_w_cb(cb):
        i = wslot_i[0] % 6
        wslot_i[0] += 1
        dma(wsl[i][:, 0:KC * 128], winb[cb * 128:(cb + 1) * 128, :], [wsl[i]], ["winb"], f"d_w{i}")
        return wsl[i]

    def proj_mm(wt, bank, N):
        for kc in range(KC):
            mm(bank[:, 0:N], wt[:, kc * 128:(kc + 1) * 128], uT[:, kc, 0:N], [bank], [wt, uT],
               start=(kc == 0), stop=(kc == KC - 1))
            yield

    def proj(cb, bank, N):
        wt = load_w_cb(cb)
        for _ in proj_mm(wt, bank, N):
            pass

    def projA_gen(cb0, N):
        wts = [load_w_cb(cb0 + i) for i in range(4)]
        for i in range(4):
            yield from proj_mm(wts[i], PJ[i], N)

    def pump(gen, n):
        if gen is None:
            return
        for _ in range(n):
            try:
                next(gen)
            except StopIteration:
                return

    def shift_mix(bank, N, mucol, carry, out):
        cp("act", pb[:, 1:N + 1], bank[:, 0:N], [pb], [bank])
        cp("pool", pb[:, 0:1], carry, [pb], [car])
        ts("dve", tmp[:, 0:N], pb[:, 0:N], pv[:, mucol:mucol + 1], ALU.mult, [tmp], [pb, pv])
        stt(out[:, 0:N], pb[:, 1:N + 1], om[:, mucol:mucol + 1], tmp[:, 0:N], ALU.mult, ALU.add, [out], [pb, om, tmp])
        cp("pool", carry, pb[:, N:N + 1], [car], [pb])

    def phase1_tile(ti):
        nch = 1 if ti == 0 else 4
        N = nch * 128
        if ti == 0:
            dma(uT[:, :, 0:128], u_meta.ap().rearrange("p (k t) -> p k t", t=128), [uT], ["u_meta"], "d_ld")
        else:
            rb = (ti - 1) * 128
            dma(uT[:].rearrange("p k t -> p (k t)"), u_loc[rb:rb + 128, :], [uT], ["u_loc"], "d_ld")
        proj(0, PJ[0], N); proj(1, PJ[1], N)
        shift_mix(PJ[0], N, 0, car[:, 0, 0:1], zwm)
        act(tw[:, 0:N], zwm[:, 0:N], AF.Tanh, [tw], [zwm])
        shift_mix(PJ[1], N, 1, car[:, 1, 0:1], zwm)
        cp("act", zab[:, 0:N], zwm[:, 0:N], [zab], [zwm])
        for hp in range(HPC):
            pb0 = 2 + NPAR * hp
            cb0 = 2 + 8 * hp
            cr = 2 + 4 * hp
            hs = slice(hp * 128, (hp + 1) * 128)
            if hp == 0:
                for i in range(4):
                    proj(cb0 + i, PJ[i], N)
            shift_mix(PJ[0], N, pb0 + 0, car[:, cr + 0, 0:1], rz)
            shift_mix(PJ[1], N, pb0 + 1, car[:, cr + 1, 0:1], kz)
            shift_mix(PJ[2], N, pb0 + 2, car[:, cr + 2, 0:1], vz)
            act(sgA[:, 0:N], PJ[3][:, 0:N], AF.Sigmoid, [sgA], [PJ[3]])
            mm(PJ[0][:, 0:N], w2b[:, hs], tw[:, 0:N], [PJ[0]], [w2b, tw])
            act(Sx[:, 0:N], PJ[0][:, 0:N], AF.Sigmoid, [Sx], [PJ[0], pv], bias=pv[:, pb0 + 3:pb0 + 4])
            mm(PJ[1][:, 0:N], a2b[:, hs], zab[:, 0:N], [PJ[1]], [a2b, zab])
            act(aa[:, 0:N], PJ[1][:, 0:N], AF.Sigmoid, [aa], [PJ[1], pv], bias=pv[:, pb0 + 4:pb0 + 5])
            for ch in range(nch):
                cs = slice(ch * 128, (ch + 1) * 128)
                P.emit("dve", lambda e, cs=cs: e.tensor_tensor_scan(Sc[:, cs], ones[:, cs], Sx[:, cs], 0.0, ALU.mult, ALU.add),
                       w=[Sc], r=[ones, Sx])
            tt("pool", Sx[:, 0:N], Sc[:, 0:N], Sx[:, 0:N], ALU.subtract, [Sx], [Sc, Sx])
            act(eP[:, 0:N], Sc[:, 0:N], AF.Exp, [eP], [Sc], scale=-CDEC)
            act(ePx[:, 0:N], Sx[:, 0:N], AF.Exp, [ePx], [Sx], scale=-CDEC)
            act(eN[:, 0:N], Sc[:, 0:N], AF.Exp, [eN], [Sc], scale=CDEC)
            act(eEnd[:, 0:nch], Sc[:, 0:N].rearrange("p (c t) -> p c t", t=128)[:, :, 127], AF.Exp, [eEnd], [Sc], scale=-CDEC)
            ts("dve", kkr[:, 0:N], kz[:, 0:N], pv[:, pb0 + 5:pb0 + 6], ALU.mult, [kkr], [kz, pv])
            tt("pool", sqb[:, 0:N], kkr[:, 0:N], kkr[:, 0:N], ALU.mult, [sqb], [kkr])
            mm(PJ[2][:, 0:N], blk1b, sqb[:, 0:N], [PJ[2]], [cstb, sqb])
            act(t2[:, 0:N], PJ[2][:, 0:N], AF.Sqrt, [t2], [PJ[2]])
            ts("dve", t2[:, 0:N], t2[:, 0:N], 1e-12, ALU.max, [t2], [t2])
            P.emit("dve", lambda e: e.reciprocal(t3[:, 0:N], t2[:, 0:N]), w=[t3], r=[t2])
            tt("dve", kk[:, 0:N], kkr[:, 0:N], t3[:, 0:N], ALU.mult, [kk], [kkr, t3])
            ts("dve", t2[:, 0:N], aa[:, 0:N], pv[:, pb0 + 6:pb0 + 7], ALU.mult, [t2], [aa, pv, om],
               s2=om[:, pb0 + 6:pb0 + 7], op1=ALU.add)
            tt("pool", k2[:, 0:N], kz[:, 0:N], t2[:, 0:N], ALU.mult, [k2], [kz, t2])
            stt(bn[:, 0:N], aa[:, 0:N], -1.0, kk[:, 0:N], ALU.mult, ALU.mult, [bn], [aa, kk])
            v3 = lambda a: a[:, 0:N].rearrange("p (c t) -> p c t", t=128)
            tt("dve", KR[:, 0:nch, 0, :], v3(kk), v3(ePx), ALU.mult, [KR], [kk, ePx])
            tt("pool", KR[:, 0:nch, 1, :], v3(rz), v3(eP), ALU.mult, [KR], [rz, eP])
            tt("dve", kt[:, 0:N], k2[:, 0:N], eN[:, 0:N], ALU.mult, [kt], [k2, eN])
            tt("pool", bnt[:, 0:N], bn[:, 0:N], eN[:, 0:N], ALU.mult, [bnt], [bn, eN])
            cp("act", vb[:, 0:N], vz[:, 0:N], [vb], [vz])
            for ch in range(nch):
                cs = slice(ch * 128, (ch + 1) * 128)
                ts("dve", kh[:, cs], kt[:, cs], eEnd[:, ch:ch + 1], ALU.mult, [kh], [kt, eEnd])
                ts("pool", bh[:, cs], bnt[:, cs], eEnd[:, ch:ch + 1], ALU.mult, [bh], [bnt, eEnd])
            stt(t3[:, 0:N], rz[:, 0:N], pv[:, pb0 + 7:pb0 + 8], k2[:, 0:N], ALU.mult, ALU.mult, [t3], [rz, pv, k2])
            mm(PJ[3][:, 0:N], blk1, t3[:, 0:N], [PJ[3]], [cst, t3])
            tt("dve", bon[:, 0:N], vz[:, 0:N], PJ[3][:, 0:N], ALU.mult, [bon], [vz, PJ[3]])
            for i in range(4):
                proj(cb0 + 4 + i, PJ[i], N)
            cp("act", hb[:, 0:N], PJ[2][:, 0:N], [hb], [PJ[2]])
            cp("pool", ub[:, 0:2], car[:, cr + 3, 0:2], [ub], [car])
            tt("dve", ub[:, 2:N + 2], hb[:, 0:N], PJ[1][:, 0:N], ALU.mult, [ub], [hb, PJ[1]])
            ts("dve", acc[:, 0:N], ub[:, 0:N], pv[:, pb0 + 10:pb0 + 11], ALU.mult, [acc], [ub, pv])
            stt(acc[:, 0:N], ub[:, 1:N + 1], pv[:, pb0 + 11:pb0 + 12], acc[:, 0:N], ALU.mult, ALU.add, [acc], [ub, pv, acc])
            stt(acc[:, 0:N], ub[:, 2:N + 2], pv[:, pb0 + 12:pb0 + 13], acc[:, 0:N], ALU.mult, ALU.add, [acc], [ub, pv, acc])
            cp("pool", car[:, cr + 3, 0:2], ub[:, N:N + 2], [car], [ub])
            tt("dve", acc[:, 0:N], acc[:, 0:N], PJ[0][:, 0:N], ALU.mult, [acc], [acc, PJ[0]])
            act(hb[:, 0:N], PJ[3][:, 0:N], AF.Sigmoid, [hb], [PJ[3]])
            tt("pool", mB[:, 0:N], acc[:, 0:N], hb[:, 0:N], ALU.mult, [mB], [acc, hb])
            gen = projA_gen(cb0 + 8, N) if hp + 1 < HPC else None
            for e_ in range(2):
                mc = blk1[:, e_ * 64:e_ * 64 + 1]
                ts("pool", KRm[:, 0:nch, e_, :], KR[:, 0:nch, 0, :], mc, ALU.mult, [KRm], [KR, cst])
                ts("dve", bntm[:, e_, 0:N], bnt[:, 0:N], mc, ALU.mult, [bntm], [bnt, cst])
                ts("pool", ktm[:, e_, 0:N], kt[:, 0:N], mc, ALU.mult, [ktm], [kt, cst])
            for ch in range(nch):
                if SKIP_SCAN:
                    break
                cs = slice(ch * 128, (ch + 1) * 128)
                for i, src in enumerate([KR[:, ch, 0, :], vb[:, cs], kh[:, cs], bh[:, cs]]):
                    trp(pstr[:, i * 128:(i + 1) * 128], src, identb, [pstr], [src, cstb])
                cp("act", TM[:].rearrange("p a b -> p (a b)"), pstr[:, 0:512], [TM], [pstr])
                krc = KR[:, ch, :, :].rearrange("p a b -> p (a b)")
                for e_ in range(2):
                    mm(SC[:, e_ * 128:(e_ + 1) * 128], KRm[:, ch, e_, :], bnt[:, cs], [SC], [KRm, bnt])
                    mm(SA[:, e_ * 256:(e_ + 1) * 256], bntm[:, e_, cs], krc, [SA], [KR, bntm])
                    mm(SB_[:, e_ * 256:(e_ + 1) * 256], ktm[:, e_, cs], krc, [SB_], [KR, ktm])
                pump(gen, NPUMP)
                sa4 = SA[:, :].rearrange("p (e s t) -> p e s t", e=2, s=2)
                sc3 = SC[:, 0:256].rearrange("p (e t) -> p e t", e=2)
                for e_ in range(2):
                    tt("dve", N_[0][:, e_, :], sc3[:, e_, :], m_sl, ALU.mult, [N_[0]], [SC, cstb])
                    tt("dve", Q_[0][:, e_, 0, :], sa4[:, e_, 0, :], m_su, ALU.mult, [Q_[0]], [SA, cstb])
                    tt("dve", ArT[:, e_, :], sa4[:, e_, 1, :], m_ui, ALU.mult, [ArT], [SA, cstb])
                    tt("dve", BB[:, e_, :, :].rearrange("p a b -> p (a b)"), SB_[:, e_ * 256:(e_ + 1) * 256], m_suui, ALU.mult, [BB], [SB_, cstb])
                for e_ in range(2):
                    tt("pool", Q_[1][:, e_, 1, :], Q_[0][:, e_, 0, :], identb, ALU.add, [Q_[1]], [Q_[0], cstb])
                for e_ in range(2):
                    mm(SA[:, e_ * 256:e_ * 256 + 128], N_[0][:, e_, :], Q_[0][:, e_, 0, :], [SA], [N_[0], Q_[0]])
                    mm(SC[:, e_ * 128:(e_ + 1) * 128], Q_[0][:, e_, 0, :], N_[0][:, e_, :], [SC], [N_[0], Q_[0]])
                pump(gen, NPUMP)
                cp("act", Q_[1][:, :, 0, :], sa4[:, :, 0, :], [Q_[1]], [SA])
                cp("dve", N_[1][:], sc3, [N_[1]], [SC])
                cur = 1
                for lev in range(1, 7):
                    nxt = 1 - cur
                    last = lev == 6
                    for e_ in range(2):
                        mm(SA[:, e_ * 256:(e_ + 1) * 256], N_[cur][:, e_, :], Q_[cur][:, e_, :, :].rearrange("p a b -> p (a b)"),
                           [SA], [N_[cur], Q_[cur]])
                        if not last:
                            mm(SC[:, e_ * 128:(e_ + 1) * 128], Q_[cur][:, e_, 0, :], N_[cur][:, e_, :], [SC], [N_[cur], Q_[cur]])
                    pump(gen, NPUMP)
                    tt("dve", Q_[nxt][:, :, 1, :], sa4[:, :, 1, :], Q_[cur][:, :, 1, :], ALU.add, [Q_[nxt]], [SA, Q_[cur]])
                    if not last:
                        cp("act", Q_[nxt][:, :, 0, :], sa4[:, :, 0, :], [Q_[nxt]], [SA])
                        cp("act", N_[nxt][:], sc3, [N_[nxt]], [SC])
                    cur = nxt
                TTf = Q_[cur]
                cp("pool", XK[:, :, 0:64], TM[:, 0, :].rearrange("p (e j) -> p e j", e=2), [XK], [TM])
                for e_ in range(2):
                    mm(SC[:, 256 + e_ * 64:256 + (e_ + 1) * 64], BB[:, e_, 0, :], TM[:, 1, e_ * 64:(e_ + 1) * 64], [SC], [BB, TM])
                pump(gen, NPUMP)
                cp("act", XK[:, :, 64:128], SC[:, 256:384].rearrange("p (e j) -> p e j", e=2), [XK], [SC])
                for e_ in range(2):
                    mm(SB_[:, e_ * 128:(e_ + 1) * 128], TTf[:, e_, 1, :], XK[:, e_, :], [SB_], [TTf, XK])
                pump(gen, NPUMP)
                sb3 = SB_[:, 0:256].rearrange("p (e x) -> p e x", e=2)
                cp("act", WZ[:, 0, :].rearrange("p (e j) -> p e j", e=2), sb3[:, :, 0:64], [WZ], [SB_])
                cp("dve", WZ[:, 1, :].rearrange("p (e j) -> p e j", e=2), sb3[:, :, 64:128], [WZ], [SB_])
                Wall = WZ[:, 0, :]
                Zall = WZ[:, 1, :]
                mm(SA[:, 0:128], Wall, TM[:, 3, :], [SA], [WZ, TM])
                mm(SA[:, 128:256], TM[:, 3, :], Zall, [SA], [WZ, TM], start=True, stop=False)
                mm(SA[:, 128:256], TM[:, 2, :], TM[:, 1, :], [SA], [TM], start=False, stop=True)
                pump(gen, NPUMP)
                tt("dve", t4[:, 0:128], SA[:, 0:128], blk1, ALU.mult, [t4], [SA, cst])
                stt(GT[:], ident, eEnd[:, ch:ch + 1], t4[:, 0:128], ALU.mult, ALU.add, [GT], [cst, eEnd, t4])
                tt("dve", Hs[:], SA[:, 128:256], blk1, ALU.mult, [Hs], [SA, cst])
                for e_ in range(2):
                    mm(SB_[:, 256 + e_ * 128:256 + (e_ + 1) * 128], Wall, ArT[:, e_, :], [SB_], [WZ, ArT])
                for e_ in range(2):
                    pr = slice(e_ * 64, (e_ + 1) * 64)
                    tt("dve", RpT[pr, :], SB_[pr, 256 + e_ * 128:256 + (e_ + 1) * 128], KR[pr, ch, 1, :], ALU.add, [RpT], [SB_, KR])
                pump(gen, NPUMP)
                for e_ in range(2):
                    reg = SC[:, e_ * 128:(e_ + 1) * 128]
                    mm(reg, TM[:, 1, :], BB[:, e_, 1, :], [SC], [TM, BB], start=True, stop=False)
                    mm(reg, Zall, ArT[:, e_, :], [SC], [WZ, ArT], start=False, stop=False)
                    mm(reg, Mb[:, hp, :], RpT[:, :], [SC], [Mb, RpT], start=False, stop=True)
                for e_ in range(2):
                    pr = slice(e_ * 64, (e_ + 1) * 64)
                    cp("act", yy[pr, cs], SC[pr, e_ * 128:(e_ + 1) * 128], [yy], [SC])
                pump(gen, NPUMP)
                mm(SA[:, 256:384], GT[:], Mb[:, hp, :], [SA], [GT, Mb])
                pump(gen, NPUMP)
                tt("dve", M32[:, hp, :], SA[:, 256:384], Hs[:], ALU.add, [M32], [SA, Hs])
                cp("act", Mb[:, hp, :], M32[:, hp, :], [Mb], [M32])
            if SKIP_SCAN:
                P.emit("pool", lambda e: e.memset(yy[:], 0.0), w=[yy])
            pump(gen, 100000)
            mm(SA[:, 0:N], blka, yy[:, 0:N], [SA], [cst, yy])
            tt("dve", yc[:, 0:N], yy[:, 0:N], SA[:, 0:N], ALU.subtract, [yc], [yy, SA])
            tt("pool", t2[:, 0:N], yc[:, 0:N], yc[:, 0:N], ALU.mult, [t2], [yc])
            mm(SB_[:, 0:N], blka, t2[:, 0:N], [SB_], [cst, t2])
            act(t2[:, 0:N], SB_[:, 0:N], AF.Sqrt, [t2], [SB_, eps2], bias=eps2[:, 0:1])
            P.emit("dve", lambda e: e.reciprocal(kkr[:, 0:N], t2[:, 0:N]), w=[kkr], r=[t2])
            tt("dve", yc[:, 0:N], yc[:, 0:N], kkr[:, 0:N], ALU.mult, [yc], [yc, kkr])
            ts("dve", yc[:, 0:N], yc[:, 0:N], pv[:, pb0 + 8:pb0 + 9], ALU.mult, [yc], [yc, pv],
               s2=pv[:, pb0 + 9:pb0 + 10], op1=ALU.add)
            tt("pool", yc[:, 0:N], yc[:, 0:N], bon[:, 0:N], ALU.add, [yc], [yc, bon])
            tt("dve", mA[:, 0:N], yc[:, 0:N], sgA[:, 0:N], ALU.mult, [mA], [yc, sgA])
            if ti > 0:
                q, off = divmod((ti - 1) * 512, QT)
                r0 = (q * c.NT2 + off // 512) * 128
                dma(m_loc[r0:r0 + 128, (2 * hp) * 512:(2 * hp + 1) * 512], mA[:, 0:512], ["m_loc"], [mA], "d_st")
                dma(m_loc[r0:r0 + 128, (2 * hp + 1) * 512:(2 * hp + 2) * 512], mB[:, 0:512], ["m_loc"], [mB], "d_st")

    for ti in range(c.NT1 + 1):
        if stop_after == 1.0:
            break
        if stop_after == 1.2 and ti == 1:
            break
        if stop_after in (1.4, 1.5) and ti == 2:
            break
        phase1_tile(ti)
        if ti > 0:
            _R, _C = 4 * c.NT2 * 128, 2 * HPC * 512
            _rc = rcof(_R, _C)
            _ka, _kb = ((ti - 1) * 128 + _rc - 1) // _rc, (ti * 128) // _rc
            if ti == c.NT1:
                _kb = _R // _rc
            if _kb > _ka:
                gather_chunked(m_loc, m_all, _R, _C, _ka, _kb)
        if ti == 0:
            gather_chunked(wo_loc, wo_all, c.NBO * 128, c.KOl * 512)
            gather_chunked(wu_loc, wu_all, c.HPR * c.NSP * c.NKH * 128, c.PKu * c.SPW)
            gather_chunked(wd_loc, wd_all, c.HPR * c.NBO * 128, c.FRC * 512)
    P.finish_phase()
    P.replay(nc, sems)
    s1.close()
    if 1 <= stop_after < 2:
        es.close()
        return nc

    s2 = ExitStack()
    h = sb(s2, "h", [128, 4, D])
    RB = sb(s2, "RB", [128, c.KO * 512], BF16)
    mT = RB[:, :].rearrange("p (k t) -> p k t", t=512)
    uT2 = RB[:, 0:KC * 512].rearrange("p (k t) -> p k t", t=512)
    hidT = RB[:, KC * 512:(KC + c.FRC) * 512].rearrange("p (k t) -> p k t", t=512)
    junk2 = RB[:, (KC + c.FRC) * 512:(KC + c.FRC) * 512 + D]
    assert (KC + c.FRC) * 512 + D <= c.KO * 512
    ALIAS = ["mT", "uT2", "hidT", "junk2"]
    fsc = sb(s2, "fsc", [128, 1])
    def fence():
        P.emit("dve", lambda e: e.memset(fsc[:], 0.0), w=ALIAS + [fsc])
    rtmp = sb(s2, "rtmp", [128, 512])
    gfb = sb(s2, "gfb", [128, 1024])
    ss2 = sb(s2, "ss2", [128, 1]); rt2 = sb(s2, "rt2", [128, 1]); rstd2 = sb(s2, "rstd2", [128, 4])
    diag2 = sb(s2, "diag2", [128, 128])
    wp = [sb(s2, f"wp{i}", [128, c.WS], BF16) for i in range(2)]
    wpi = [0]
    qcache = {}

    def wslot():
        i = wpi[0] % 2
        wpi[0] += 1
        return wp[i], f"d_w{i}"

    def phase2_tile(tt_):
        t0 = tt_ * 512
        if stop_after in (2.0, 2.01):
            return
        for s in range(4):
            dma(h[:, s, :], x_d[t0 + s * 128:t0 + (s + 1) * 128, :], [h], [], "d_x")
        if stop_after == 2.05:
            return

        ld_eng = "sp" if tt_ < 2 else "act"

        Rm, Cm = 4 * c.NT2 * 128, 2 * HPC * 512
        rcm = rcof(Rm, Cm)
        nseg = max(1, 128 // rcm)
        assert rcm >= 128 and rcm % 128 == 0 or 128 % rcm == 0

        def ld_m(e):
            if ld_eng not in qcache:
                qcache[ld_eng] = (e.partition_id() % 4)
            qq = qcache[ld_eng]
            last = None
            if rcm >= 128:
                bpc = rcm // 128
                assert c.NT2 % bpc == 0 or bpc % c.NT2 == 0
                for g in range(4):
                    if bpc >= 4 * c.NT2:
                        base = g * rcm + tt_ * 128
                        dyn = qq * (c.NT2 * 128)
                        span = 3 * c.NT2 * 128
                    else:
                        assert c.NT2 % bpc == 0
                        base = (tt_ // bpc) * 4 * rcm + g * rcm + (tt_ % bpc) * 128
                        dyn = qq * ((c.NT2 // bpc) * 4 * rcm)
                        span = 3 * (c.NT2 // bpc) * 4 * rcm
                    win = m_all.ap()[base:base + span + 128, :]
                    if last is not None:
                        last.then_inc(sems["d_ld"], 16)
                    last = e.dma_start(out=RB[:, g * Cm:(g + 1) * Cm], in_=win[bass.ds(dyn, 128), :])
                return last, 4
            mv = m_all.ap().rearrange("(k g r) c -> k g r c", g=4, r=rcm)
            for h_ in range(nseg):
                k0 = tt_ * nseg + h_
                win = mv[k0:k0 + 3 * c.NT2 * nseg + 1]
                src = win[bass.ds(qq * (c.NT2 * nseg), 1)].rearrange("k g r c -> (k r) g c")
                if last is not None:
                    last.then_inc(sems["d_ld"], 16)
                last = e.dma_start(out=RB[h_ * rcm:(h_ + 1) * rcm, :].rearrange("p (g c) -> p g c", g=4), in_=src)
            return last, nseg

        ndm = 4 if rcm >= 128 else nseg
        P.emit(ld_eng, lambda e: ld_m(e)[0], w=["mT"], r=["m_all"], sig="d_ld", inc=16)
        P.cnt["d_ld"] += 16 * (ndm - 1)
        if stop_after == 2.1:
            return
        for n in range(c.NBO):
            for kq in range(c.NKQ):
                wt, sg = wslot()
                for rr in range(c.RPP):
                    rk = kq * c.RPP + rr
                    gload(wt, rr * c.KOl * 512, c.KOl * 512, wo_all, rk, n * 128, c.NBO * 128, c.KOl * 512, sg)
                for s in range(4):
                    for k in range(c.PK):
                        kcg = kq * c.PK + k
                        mm(PJ[s][:, :], mT[:, kcg, s * 128:(s + 1) * 128], wt[:, k * 512:(k + 1) * 512], [PJ[s]], ["mT", wt],
                           start=(kq == 0 and k == 0), stop=(kq == c.NKQ - 1 and k == c.PK - 1))
            for s in range(4):
                tt("dve", h[:, s, n * 512:(n + 1) * 512], h[:, s, n * 512:(n + 1) * 512], PJ[s][:, :], ALU.add, [h], [h, PJ[s]])
        if stop_after == 2.2:
            return
        fence()
        for s in range(4):
            rms_to_uT(h[:, s, :], GMLP, uT2, s * 128, junk2, ss2[:], rt2[:], rstd2[:, s:s + 1], diag2[:], uTk="uT2", junkk="junk2")
        nhc = c.SPW // 128
        for j in range(8):
            for sp_ in range(c.NSP):
                for kh_ in range(c.NKH):
                    wt, sg = wslot()
                    gload(wt, 0, c.PKu * c.SPW, wu_all, j // c.HPR, (((j % c.HPR) * c.NSP + sp_) * c.NKH + kh_) * 128,
                          c.HPR * c.NSP * c.NKH * 128, c.PKu * c.SPW, sg)
                    for hc in range(nhc):
                        for k in range(c.PKu):
                            kc = kh_ * c.PKu + k
                            mm(PJ[hc][:, :], wt[:, k * c.SPW + hc * 128:k * c.SPW + (hc + 1) * 128], uT2[:, kc, :], [PJ[hc]], [wt, "uT2"],
                               start=(kh_ == 0 and k == 0), stop=(kh_ == c.NKH - 1 and k == c.PKu - 1))
                for hc in range(nhc):
                    act(rtmp[:], PJ[hc][:, :], AF.Relu, [rtmp], [PJ[hc]])
                    tt("pool", hidT[:, sp_ * nhc + hc, :], rtmp[:], rtmp[:], ALU.mult, ["hidT"], [rtmp])
            for n in range(c.NBO):
                wt, sg = wslot()
                gload(wt, 0, c.FRC * 512, wd_all, j // c.HPR, ((j % c.HPR) * c.NBO + n) * 128, c.HPR * c.NBO * 128, c.FRC * 512, sg)
                for s in range(4):
                    for k in range(c.FRC):
                        mm(PJ[s][:, :], hidT[:, k, s * 128:(s + 1) * 128], wt[:, k * 512:(k + 1) * 512], [PJ[s]], ["hidT", wt],
                           start=(k == 0), stop=(k == c.FRC - 1))
                for s in range(4):
                    tt("dve", h[:, s, n * 512:(n + 1) * 512], h[:, s, n * 512:(n + 1) * 512], PJ[s][:, :], ALU.add, [h], [h, PJ[s]])
        if stop_after == 2.3:
            return
        for s in range(4):
            act(junk2[:], h[:, s, :], AF.Square, ["junk2", ss2], [h], accum=ss2[:])
            act(rt2[:], ss2[:], AF.Sqrt, [rt2], [ss2, eps1], bias=eps1[:, 0:1], scale=1.0 / D)
            P.emit("dve", lambda e, s=s: e.reciprocal(rstd2[:, s:s + 1], rt2[:]), w=[rstd2], r=[rt2])
        for cq in range(D // 1024 if D >= 1024 else 1):
            cw = min(1024, D)
            dma(gfb[:, 0:cw], gf_d[:, cq * cw:(cq + 1) * cw].partition_broadcast(128), [gfb], [], "d_ld")
            for s in range(4):
                stt(h[:, s, cq * cw:(cq + 1) * cw], h[:, s, cq * cw:(cq + 1) * cw], rstd2[:, s:s + 1], gfb[:, 0:cw],
                    ALU.mult, ALU.mult, [h], [h, rstd2, gfb])
        fence()
        for s in range(4):
            dma(out_d[t0 + s * 128:t0 + (s + 1) * 128, :], h[:, s, :], ["out"], [h], "d_out")

    for tt_ in range(c.NT2):
        phase2_tile(tt_)
    P.finish_phase()
    if stop_after != 2.01:
        P.replay(nc, sems)
    s2.close()
    es.close()
    return nc


def make_consts():
    cst = np.zeros((128, 1024), np.float32)
    cst[:, 0:128] = np.eye(128)
    p = np.arange(128)
    blk = (p[:, None] // 64 == p[None, :] // 64).astype(np.float32)
    cst[:, 128:256] = blk
    cst[:, 256:384] = blk / 64.0
    cst[:, 384:448] = (p[:, None] % 64 == np.arange(64)[None, :]).astype(np.float32)
    cst[:, 512:640] = (p[None, :] < p[:, None])
    cst[:, 640:768] = (p[None, :] > p[:, None])
    cst[:, 768:896] = (p[None, :] >= p[:, None])
    return cst


def prep_inputs(cfg, inp):
    c = cfg
    D, KC, HPC = c.D, c.KC, c.HPC
    f = lambda a: np.ascontiguousarray(np.asarray(a, dtype=np.float32))
    x = f(inp["x"]); w_in = f(inp["w_in"])[0]; w_out = f(inp["w_out"])[0]
    w_up = f(inp["w_up"])[0]; w_down = f(inp["w_down"])[0]
    mu = f(inp["rwkv_shift_mu"])[0]
    w0 = f(inp["rwkv_w0"])[0]; a0 = f(inp["rwkv_a0"])[0]; k_k = f(inp["rwkv_k_k"])[0]; k_a = f(inp["rwkv_k_a"])[0]
    r_k = f(inp["rwkv_r_k"])[0].reshape(-1); ln_w = f(inp["rwkv_ln_w"])[0]; ln_b = f(inp["rwkv_ln_b"])[0]
    cw = f(inp["conv_w"])[0]
    w2 = f(inp["rwkv_w2"])[0]; a2 = f(inp["rwkv_a2"])[0]
    g_mix = f(inp["norm_mix_g"])[0]; g_mlp = f(inp["norm_mlp_g"])[0]; gf = f(inp["norm_final_g"]).reshape(1, D)
    meta = f(inp["meta_tokens"])
    NR = c.NR
    cst = make_consts()
    perm = []
    for g in range(4):
        for hp in range(HPC):
            gp = g * HPC + hp
            perm.append(np.arange(gp * 128, (gp + 1) * 128))
            perm.append(D + np.arange(gp * 128, (gp + 1) * 128))
    perm = np.concatenate(perm)
    w_out_p = w_out[perm]
    maps = []
    for core in range(8):
        b, g = divmod(core, 4)
        q = g
        cols = [np.arange(3 * D, 3 * D + 128), np.arange(3 * D + 128, 3 * D + 256)]
        pvc = [mu[3 * D:3 * D + 128], mu[3 * D + 128:3 * D + 256]]
        for hp in range(HPC):
            gp = g * HPC + hp
            ch = np.arange(gp * 128, (gp + 1) * 128)
            cols += [ch, D + ch, 2 * D + ch, NR + 3 * D + ch, NR + ch, NR + D + ch, NR + 2 * D + ch, NR + 4 * D + ch]
            pvc += [mu[ch], mu[D + ch], mu[2 * D + ch], w0[ch], a0[ch], k_k[ch], k_a[ch], r_k[ch], ln_w[ch], ln_b[ch],
                    cw[0][ch], cw[1][ch], cw[2][ch]]
        pvm = np.stack(pvc, axis=1)
        pvm = np.concatenate([pvm, g_mix.reshape(KC, 128).T, g_mlp.reshape(KC, 128).T], axis=1)
        cols = np.concatenate(cols)
        ws = w_in[:, cols]
        win = ws.reshape(KC, 128, c.NCB, 128).transpose(2, 1, 0, 3).reshape(c.NCB * 128, KC * 128)
        chs = np.arange(g * HPC * 128, (g + 1) * HPC * 128)
        r = g
        wo_r = w_out_p[r * c.KOl * 128:(r + 1) * c.KOl * 128]
        wo = wo_r.reshape(c.KOl, 128, c.NBO, 512).transpose(2, 1, 0, 3).reshape(c.NBO * 128, c.KOl * 512)
        wu_r = w_up[:, r * c.HPR * c.FR:(r + 1) * c.HPR * c.FR]
        wu = wu_r.reshape(c.NKH, c.PKu, 128, c.HPR, c.NSP, c.SPW).transpose(3, 4, 0, 2, 1, 5).reshape(c.HPR * c.NSP * c.NKH * 128, c.PKu * c.SPW)
        wd_r = w_down[r * c.HPR * c.FR:(r + 1) * c.HPR * c.FR]
        wd = wd_r.reshape(c.HPR, c.FRC, 128, c.NBO, 512).transpose(0, 3, 2, 1, 4).reshape(c.HPR * c.NBO * 128, c.FRC * 512)
        maps.append({
            "x": np.ascontiguousarray(x[b, q * c.QT:(q + 1) * c.QT]), "xb": np.ascontiguousarray(x[b]),
            "meta": meta, "win": np.ascontiguousarray(win), "wo": np.ascontiguousarray(wo),
            "wu": np.ascontiguousarray(wu), "wd": np.ascontiguousarray(wd),
            "pv": np.ascontiguousarray(pvm), "w2": np.ascontiguousarray(w2[:, chs]), "a2": np.ascontiguousarray(a2[:, chs]),
            "gf": gf, "cst": cst,
        })
    return maps


def run_cfg(cfg, inp, stop_after=9, skip_scan=False):
    nc = build(cfg, stop_after, skip_scan)
    maps = prep_inputs(cfg, inp)
    res = run_bass_kernel_spmd(nc, maps, core_ids=list(range(8)))
    out = np.zeros((2, cfg.SEQ, cfg.D), np.float32)
    for core in range(8):
        b, q = divmod(core, 4)
        out[b, q * cfg.QT:(q + 1) * cfg.QT] = res.results[core]["out"]
    return out


def kernel(**inputs):
    return run_cfg(FULL, inputs)
```
